# Optimizing a Trainium2 kernel written in Bass

```python
import jax, jax.numpy as jnp
from jax import lax
import numpy as np

D_MODEL = 1024
BATCH = 32
SEQ = 2048
DEPTH = 2
DEC_BATCH = 32
DEC_SEQ = 32
PAST_LEN = 2048

CHUNK = 64
HEAD_DIM = 64
N_HEADS_A = D_MODEL // HEAD_DIM // 2
N_HEADS_B = D_MODEL // HEAD_DIM // 2
N_HEADS_C = D_MODEL // HEAD_DIM
A_LEFT_CHUNKS = 8
A_PAST = A_LEFT_CHUNKS * CHUNK
A_BAND = A_PAST + CHUNK
REL_CLIP = 128
Q_BLOCK = 128
N_MEM = 256
N_HEADS_X = 4
HEAD_DIM_X = D_MODEL // N_HEADS_X
D_FF = 4 * D_MODEL
N_AB = (DEPTH + 1) // 2
N_SB = DEPTH // 2
FORGET_BIAS_INIT = 3.0
EPS = 1e-6
NEG = -1e30

kernel_name = 'hybrid_chunk_stream_encoder_step'


def rmsnorm(x, g):
    xf = x.astype(jnp.float32)
    y = xf * lax.rsqrt(jnp.mean(xf * xf, axis=-1, keepdims=True) + EPS)
    return (y * g.astype(jnp.float32)).astype(x.dtype)


def band_gather(t):
    b, s, h, d = t.shape
    nc = s // CHUNK
    tp = jnp.pad(t, ((0, 0), (A_PAST, 0), (0, 0), (0, 0)))
    tc = tp.reshape(b, nc + A_LEFT_CHUNKS, CHUNK, h, d)
    band = jnp.stack([tc[:, i:i + nc] for i in range(A_LEFT_CHUNKS + 1)], axis=2)
    return band.reshape(b, nc, A_BAND, h, d)


def chunk_band_attend(q, k, v, q_pos, k_pos, rel_bias):
    s = jnp.einsum('bnqhd,bnkhd->bnhqk', q, k, preferred_element_type=jnp.float32) * (HEAD_DIM ** -0.5)
    rel = jnp.clip(k_pos[:, None, :] - q_pos[:, :, None], -REL_CLIP, REL_CLIP) + REL_CLIP
    bias = jnp.transpose(rel_bias[rel], (0, 3, 1, 2)).astype(jnp.float32)
    qc = q_pos // CHUNK
    kc = k_pos // CHUNK
    ok = ((k_pos[:, None, :] >= 0) & (kc[:, None, :] <= qc[:, :, None])
          & (kc[:, None, :] >= qc[:, :, None] - A_LEFT_CHUNKS))
    s = jnp.where(ok[None, :, None], s + bias[None], NEG)
    p = jax.nn.softmax(s, axis=-1).astype(v.dtype)
    return jnp.einsum('bnhqk,bnkhd->bnqhd', p, v)


def forget_attend(q, k, v, cq, ck, q_pos, k_pos):
    s = jnp.einsum('bqhd,bkhd->bhqk', q, k, preferred_element_type=jnp.float32) * (HEAD_DIM ** -0.5)
    decay = jnp.transpose(cq, (0, 2, 1))[:, :, :, None] - jnp.transpose(ck, (0, 2, 1))[:, :, None, :]
    ok = k_pos[None, :] <= q_pos[:, None]
    s = jnp.where(ok, s + decay.astype(jnp.float32), NEG)
    p = jax.nn.softmax(s, axis=-1).astype(v.dtype)
    return jnp.einsum('bhqk,bkhd->bqhd', p, v)


def stick_breaking_attend(q, k, v, q_pos, k_pos):
    z = jnp.einsum('bqhd,bkhd->bhqk', q, k, preferred_element_type=jnp.float32) * (HEAD_DIM ** -0.5)
    before = k_pos[None, :] < q_pos[:, None]
    log_keep = jnp.where(before, jax.nn.log_sigmoid(-z), 0.0)
    tail = lax.cumsum(log_keep, axis=3, reverse=True) - log_keep
    w = jnp.where(before, jnp.exp(jax.nn.log_sigmoid(z) + tail), 0.0)
    return jnp.einsum('bhqk,bkhd->bqhd', w.astype(v.dtype), v)


def ab_project(xn, w_in, b_f, qk_gain):
    b, t, _ = xn.shape
    da = N_HEADS_A * HEAD_DIM
    db = N_HEADS_B * HEAD_DIM
    cuts = [da, 2 * da, 3 * da, 3 * da + db, 3 * da + 2 * db, 3 * da + 3 * db]
    qa, ka, va, qb, kb, vb, fl = jnp.split(xn @ w_in, cuts, axis=-1)
    qa = rmsnorm(qa.reshape(b, t, N_HEADS_A, HEAD_DIM), qk_gain[0])
    ka = rmsnorm(ka.reshape(b, t, N_HEADS_A, HEAD_DIM), qk_gain[1])
    va = va.reshape(b, t, N_HEADS_A, HEAD_DIM)
    qb = rmsnorm(qb.reshape(b, t, N_HEADS_B, HEAD_DIM), qk_gain[2])
    kb = rmsnorm(kb.reshape(b, t, N_HEADS_B, HEAD_DIM), qk_gain[3])
    vb = vb.reshape(b, t, N_HEADS_B, HEAD_DIM)
    logf = jax.nn.log_sigmoid((fl + b_f).astype(jnp.float32))
    return qa, ka, va, qb, kb, vb, logf


def ab_merge(oa, ob, w_out):
    b, t = oa.shape[:2]
    return jnp.concatenate([oa.reshape(b, t, -1), ob.reshape(b, t, -1)], axis=-1) @ w_out


def ab_prompt(xn, w_in, b_f, qk_gain, rel_bias, w_out):
    b, s, _ = xn.shape
    qa, ka, va, qb, kb, vb, logf = ab_project(xn, w_in, b_f, qk_gain)
    nc = s // CHUNK
    starts = jnp.arange(nc)[:, None] * CHUNK
    q_pos = starts + jnp.arange(CHUNK)[None, :]
    k_pos = starts - A_PAST + jnp.arange(A_BAND)[None, :]
    oa = chunk_band_attend(qa.reshape(b, nc, CHUNK, N_HEADS_A, HEAD_DIM), band_gather(ka),
                           band_gather(va), q_pos, k_pos, rel_bias)
    oa = oa.reshape(b, s, N_HEADS_A, HEAD_DIM)
    c = jnp.cumsum(logf, axis=1)
    pos = jnp.arange(s)
    blocks = []
    for j in range(s // Q_BLOCK):
        lo, hi = j * Q_BLOCK, (j + 1) * Q_BLOCK
        blocks.append(forget_attend(qb[:, lo:hi], kb[:, :hi], vb[:, :hi], c[:, lo:hi], c[:, :hi],
                                    pos[lo:hi], pos[:hi]))
    ob = jnp.concatenate(blocks, axis=1)
    keep = min(A_PAST, s)
    return ab_merge(oa, ob, w_out), (ka[:, s - keep:], va[:, s - keep:], kb, vb, logf)


def ab_sample(xn, ca_k, ca_v, cb_k, cb_v, cb_logf, w_in, b_f, qk_gain, rel_bias, w_out):
    b, t, _ = xn.shape
    qa, ka, va, qb, kb, vb, logf = ab_project(xn, w_in, b_f, qk_gain)
    n_band = ca_k.shape[1]
    past = cb_k.shape[1]
    q_pos = past + jnp.arange(t)
    ka_all = jnp.concatenate([ca_k, ka], axis=1)
    va_all = jnp.concatenate([ca_v, va], axis=1)
    k_pos = past - n_band + jnp.arange(n_band + t)
    oa = chunk_band_attend(qa[:, None], ka_all[:, None], va_all[:, None], q_pos[None], k_pos[None],
                           rel_bias)[:, 0]
    kb_all = jnp.concatenate([cb_k, kb], axis=1)
    vb_all = jnp.concatenate([cb_v, vb], axis=1)
    c = jnp.cumsum(jnp.concatenate([cb_logf.astype(jnp.float32), logf], axis=1), axis=1)
    ob = forget_attend(qb, kb_all, vb_all, c[:, past:], c, q_pos, jnp.arange(past + t))
    return ab_merge(oa, ob, w_out), (ka, va, kb, vb, logf)


def sb_project(xn, w_in):
    b, t, _ = xn.shape
    q, k, v = jnp.split(xn @ w_in, 3, axis=-1)
    shp = (b, t, N_HEADS_C, HEAD_DIM)
    return q.reshape(shp), k.reshape(shp), v.reshape(shp)


def sb_prompt(xn, w_in, w_out):
    b, s, _ = xn.shape
    q, k, v = sb_project(xn, w_in)
    pos = jnp.arange(s)
    blocks = []
    for j in range(s // Q_BLOCK):
        lo, hi = j * Q_BLOCK, (j + 1) * Q_BLOCK
        blocks.append(stick_breaking_attend(q[:, lo:hi], k[:, :hi], v[:, :hi], pos[lo:hi], pos[:hi]))
    o = jnp.concatenate(blocks, axis=1)
    return o.reshape(b, s, -1) @ w_out, (k, v)


def sb_sample(xn, cc_k, cc_v, w_in, w_out):
    b, t, _ = xn.shape
    q, k, v = sb_project(xn, w_in)
    past = cc_k.shape[1]
    k_all = jnp.concatenate([cc_k, k], axis=1)
    v_all = jnp.concatenate([cc_v, v], axis=1)
    o = stick_breaking_attend(q, k_all, v_all, past + jnp.arange(t), jnp.arange(past + t))
    return o.reshape(b, t, -1) @ w_out, (k, v)


def mem_kv(mem, g_mem, w_kv, k_gain):
    b, n, _ = mem.shape
    k, v = jnp.split(rmsnorm(mem, g_mem) @ w_kv, 2, axis=-1)
    k = rmsnorm(k.reshape(b, n, N_HEADS_X, HEAD_DIM_X), k_gain)
    return k, v.reshape(b, n, N_HEADS_X, HEAD_DIM_X)


def cross_attend(hn, k, v, w_q, q_gain, w_o):
    b, t, _ = hn.shape
    q = rmsnorm((hn @ w_q).reshape(b, t, N_HEADS_X, HEAD_DIM_X), q_gain)
    s = jnp.einsum('bqhd,bkhd->bhqk', q, k, preferred_element_type=jnp.float32) * (HEAD_DIM_X ** -0.5)
    p = jax.nn.softmax(s, axis=-1).astype(v.dtype)
    o = jnp.einsum('bhqk,bkhd->bqhd', p, v)
    return o.reshape(b, t, -1) @ w_o


def mlp(hn, w_up, w_down):
    return jnp.square(jax.nn.relu(hn @ w_up)) @ w_down


def setup_inputs(seed: int = 0) -> dict:
    key = jax.random.key(seed)
    keys = iter(jax.random.split(key, 64))

    def nrm(shape, scale=1.0):
        return jax.random.normal(next(keys), shape, jnp.float32) * scale

    def gain(shape):
        return 1.0 + 0.02 * nrm(shape)

    d = D_MODEL
    dab = (N_HEADS_A + N_HEADS_B) * HEAD_DIM
    dc = N_HEADS_C * HEAD_DIM
    dx = N_HEADS_X * HEAD_DIM_X
    a_rows = min(A_PAST, PAST_LEN)
    return {
        'x_prompt': nrm((BATCH, SEQ, d)),
        'x_sample': nrm((DEC_BATCH, DEC_SEQ, d)),
        'mem_prompt': nrm((BATCH, N_MEM, d)),
        'cache_a_k': nrm((N_AB, DEC_BATCH, a_rows, N_HEADS_A, HEAD_DIM)),
        'cache_a_v': nrm((N_AB, DEC_BATCH, a_rows, N_HEADS_A, HEAD_DIM)),
        'cache_b_k': nrm((N_AB, DEC_BATCH, PAST_LEN, N_HEADS_B, HEAD_DIM)),
        'cache_b_v': nrm((N_AB, DEC_BATCH, PAST_LEN, N_HEADS_B, HEAD_DIM)),
        'cache_b_logf': jax.nn.log_sigmoid(FORGET_BIAS_INIT + nrm((N_AB, DEC_BATCH, PAST_LEN, N_HEADS_B))),
        'cache_c_k': nrm((N_SB, DEC_BATCH, PAST_LEN, N_HEADS_C, HEAD_DIM)),
        'cache_c_v': nrm((N_SB, DEC_BATCH, PAST_LEN, N_HEADS_C, HEAD_DIM)),
        'cache_mem_k': nrm((DEPTH, DEC_BATCH, N_MEM, N_HEADS_X, HEAD_DIM_X)),
        'cache_mem_v': nrm((DEPTH, DEC_BATCH, N_MEM, N_HEADS_X, HEAD_DIM_X)),
        'norm_mix': gain((DEPTH, d)),
        'norm_cross': gain((DEPTH, d)),
        'norm_mlp': gain((DEPTH, d)),
        'norm_mem': gain((DEPTH, d)),
        'ab_w_in': nrm((N_AB, d, 3 * dab + N_HEADS_B), d ** -0.5),
        'ab_forget_bias': FORGET_BIAS_INIT + 0.1 * nrm((N_AB, N_HEADS_B)),
        'ab_qk_gain': gain((N_AB, 4, HEAD_DIM)),
        'ab_rel_bias': nrm((N_AB, 2 * REL_CLIP + 1, N_HEADS_A), 0.5),
        'ab_w_out': nrm((N_AB, dab, d), dab ** -0.5),
        'sb_w_in': nrm((N_SB, d, 3 * dc), d ** -0.5),
        'sb_w_out': nrm((N_SB, dc, d), dc ** -0.5),
        'x_w_q': nrm((DEPTH, d, dx), d ** -0.5),
        'x_w_kv': nrm((DEPTH, d, 2 * dx), d ** -0.5),
        'x_qk_gain': gain((DEPTH, 2, HEAD_DIM_X)),
        'x_w_o': nrm((DEPTH, dx, d), dx ** -0.5),
        'mlp_w_up': nrm((DEPTH, d, D_FF), d ** -0.5),
        'mlp_w_down': nrm((DEPTH, D_FF, d), D_FF ** -0.5),
    }


def reference(x_prompt, x_sample, mem_prompt, cache_a_k, cache_a_v, cache_b_k, cache_b_v, cache_b_logf,
              cache_c_k, cache_c_v, cache_mem_k, cache_mem_v, norm_mix, norm_cross, norm_mlp, norm_mem,
              ab_w_in, ab_forget_bias, ab_qk_gain, ab_rel_bias, ab_w_out, sb_w_in, sb_w_out,
              x_w_q, x_w_kv, x_qk_gain, x_w_o, mlp_w_up, mlp_w_down):
    hp, hs = x_prompt, x_sample
    a_kp, a_vp, a_ks, a_vs = [], [], [], []
    b_kp, b_vp, b_fp, b_ks, b_vs, b_fs = [], [], [], [], [], []
    c_kp, c_vp, c_ks, c_vs = [], [], [], []
    m_kp, m_vp = [], []
    for layer in range(DEPTH):
        g = norm_mix[layer]
        i = layer // 2
        if layer % 2 == 0:
            dp, (ak, av, bk, bv, bf) = ab_prompt(rmsnorm(hp, g), ab_w_in[i], ab_forget_bias[i],
                                                 ab_qk_gain[i], ab_rel_bias[i], ab_w_out[i])
            ds, (ak2, av2, bk2, bv2, bf2) = ab_sample(rmsnorm(hs, g), cache_a_k[i], cache_a_v[i],
                                                      cache_b_k[i], cache_b_v[i], cache_b_logf[i],
                                                      ab_w_in[i], ab_forget_bias[i], ab_qk_gain[i],
                                                      ab_rel_bias[i], ab_w_out[i])
            a_kp.append(ak); a_vp.append(av); a_ks.append(ak2); a_vs.append(av2)
            b_kp.append(bk); b_vp.append(bv); b_fp.append(bf)
            b_ks.append(bk2); b_vs.append(bv2); b_fs.append(bf2)
        else:
            dp, (ck, cv) = sb_prompt(rmsnorm(hp, g), sb_w_in[i], sb_w_out[i])
            ds, (ck2, cv2) = sb_sample(rmsnorm(hs, g), cache_c_k[i], cache_c_v[i], sb_w_in[i], sb_w_out[i])
            c_kp.append(ck); c_vp.append(cv); c_ks.append(ck2); c_vs.append(cv2)
        hp = hp + dp
        hs = hs + ds
        mk, mv = mem_kv(mem_prompt, norm_mem[layer], x_w_kv[layer], x_qk_gain[layer, 1])
        m_kp.append(mk); m_vp.append(mv)
        hp = hp + cross_attend(rmsnorm(hp, norm_cross[layer]), mk, mv, x_w_q[layer], x_qk_gain[layer, 0], x_w_o[layer])
        hs = hs + cross_attend(rmsnorm(hs, norm_cross[layer]), cache_mem_k[layer], cache_mem_v[layer],
                               x_w_q[layer], x_qk_gain[layer, 0], x_w_o[layer])
        hp = hp + mlp(rmsnorm(hp, norm_mlp[layer]), mlp_w_up[layer], mlp_w_down[layer])
        hs = hs + mlp(rmsnorm(hs, norm_mlp[layer]), mlp_w_up[layer], mlp_w_down[layer])
    return (hp, hs,
            jnp.stack(a_kp), jnp.stack(a_vp), jnp.stack(a_ks), jnp.stack(a_vs),
            jnp.stack(b_kp), jnp.stack(b_vp), jnp.stack(b_fp),
            jnp.stack(b_ks), jnp.stack(b_vs), jnp.stack(b_fs),
            jnp.stack(c_kp), jnp.stack(c_vp), jnp.stack(c_ks), jnp.stack(c_vs),
            jnp.stack(m_kp), jnp.stack(m_vp))
```

```python
import os
import numpy as np
import concourse.bass as bass
import concourse.mybir as mybir
from concourse.bass_utils import run_bass_kernel_spmd
from contextlib import ExitStack

F32 = mybir.dt.float32
BF16 = mybir.dt.bfloat16
AF = mybir.ActivationFunctionType
ALU = mybir.AluOpType
AX = mybir.AxisListType
NEGM = -30000.0
EPS = 1e-6


class Op:
    __slots__ = ("id", "eng", "fn", "deps", "dma", "signals", "sigidx", "dsem", "target")


class Sched:
    CAP = 30000
    NDMA = 8

    def __init__(self):
        self.ops = []
        self.buf = {}
        self.ctx = []

    def add(self, eng, fn, r=(), w=(), dma=False):
        if getattr(self, 'mute', False):
            return None
        i = len(self.ops)
        deps = set()
        nk = lambda k: (k[0], k[1]) if (isinstance(k, tuple) and len(k) > 2 and k[0] == 'ps') else k
        r = [nk(k) for k in (list(r) + self.ctx + getattr(self, 'base_ctx', []))]
        w = [nk(k) for k in w]
        w = w + [k for k in r if isinstance(k, tuple) and k[0] == 'ps' and k not in w]
        for k in r:
            st = self.buf.get(k)
            if st is not None and st[0] is not None:
                deps.add(st[0])
        for k in w:
            st = self.buf.get(k)
            if st is not None:
                if st[0] is not None:
                    deps.add(st[0])
                deps.update(st[1].values())
                deps.update(st[2])
        best = {}
        red = set()
        for d in deps:
            D = self.ops[d]
            if D.dma:
                red.add(d)
            else:
                if best.get(D.eng, -1) < d:
                    best[D.eng] = d
        red.update(best.values())
        op = Op()
        op.id = i; op.eng = eng; op.fn = fn; op.deps = red; op.dma = dma
        op.signals = False; op.sigidx = 0; op.dsem = None; op.target = 0
        self.ops.append(op)
        for k in r:
            st = self.buf.get(k)
            if st is None:
                st = [None, {}, []]
                self.buf[k] = st
            if dma:
                st[2].append(i)
            else:
                st[1][eng] = i
        for k in w:
            self.buf[k] = [i, {}, []]
        return op

    def finalize(self):
        ops = self.ops
        for op in ops:
            for d in op.deps:
                D = ops[d]
                if D.dma:
                    continue
                if D.eng == 'pe' and op.eng == 'pe' and not op.dma:
                    continue
                D.signals = True
        cnt = {}
        dcnt = {}
        for op in ops:
            if op.dma:
                j = dcnt.get(op.eng, 0)
                dcnt[op.eng] = j + 1
                op.dsem = (op.eng, j % self.NDMA)
                op.target = 16 * (j // self.NDMA + 1)
            elif op.signals:
                c = cnt.get(op.eng, 0) + 1
                cnt[op.eng] = c
                op.sigidx = c
        self.cnt = cnt
        self.dcnt = dcnt

    def n_epochs(self, eng):
        return (self.cnt.get(eng, 0) + self.CAP - 1) // self.CAP + 1

    def emit(self, eng_name, e, sems, dsems, final=False):
        ops = self.ops
        seen = {}
        seend = {}

        def wait_c(D):
            if seen.get(D.eng, 0) >= D.sigidx:
                return
            seen[D.eng] = D.sigidx
            ep = (D.sigidx - 1) // self.CAP
            v = (D.sigidx - 1) % self.CAP + 1
            e.wait_ge(sems[D.eng][ep], v)

        def wait_d(dsem, target):
            if seend.get(dsem, 0) >= target:
                return
            seend[dsem] = target
            e.wait_ge(dsems[dsem], target)

        for op in ops:
            if op.eng != eng_name:
                continue
            for d in sorted(op.deps):
                D = ops[d]
                if D.dma:
                    wait_d(D.dsem, D.target)
                else:
                    if D.eng == 'pe' and eng_name == 'pe' and not op.dma:
                        continue
                    wait_c(D)
            if op.dma and op.target > 16:
                wait_d(op.dsem, op.target - 16)
            inst = op.fn(e)
            if op.dma:
                inst.then_inc(dsems[op.dsem], 16)
            elif op.signals:
                ep = (op.sigidx - 1) // self.CAP
                inst.then_inc(sems[eng_name][ep], 1)
        if final:
            for q, n in self.dcnt.items():
                for j in range(min(n, self.NDMA)):
                    uses = (n - 1 - j) // self.NDMA + 1
                    wait_d((q, j), 16 * uses)


class Rot:
    def __init__(self, items):
        self.items = items
        self.i = 0

    def next(self):
        it = self.items[self.i % len(self.items)]
        self.i += 1
        return it


def _prod(x):
    r = 1
    for v in x:
        r *= v
    return r


class Arena:
    def __init__(self, t, nbytes):
        self.t = t
        self.nbytes = nbytes
        self.off = 0

    def alloc(self, shape, dt):
        n = _prod(shape[1:])
        bpe = 4 if dt == F32 else 2
        nb = n * bpe
        o = self.off
        self.off += (nb + 31) // 32 * 32
        assert self.off <= self.nbytes, ("arena overflow", self.off, self.nbytes)
        ap = self.t[:, o // 2:(o + nb) // 2]
        if dt == F32:
            ap = ap.bitcast(F32)
        if len(shape) == 3:
            ap = ap.rearrange("p (a b) -> p a b", a=shape[1])
        elif len(shape) == 4:
            ap = ap.rearrange("p (a b c) -> p a b c", a=shape[1], b=shape[2])
        return ap


NSEQ = int(os.environ.get("K_NSEQ", "4"))
DO_SAMPLE = int(os.environ.get("K_SAMPLE", "1"))
DO_L1 = int(os.environ.get("K_L1", "1"))
STAGE = int(os.environ.get("K_STAGE", "99"))
SETUP = int(os.environ.get("K_SETUP", "99"))
PPL = int(os.environ.get("K_PP", "99"))
XL = int(os.environ.get("K_X", "99"))
ML = int(os.environ.get("K_M", "99"))
CL = int(os.environ.get("K_C", "99"))
STAGE1 = int(os.environ.get("K_STAGE1", "99"))
SS = int(os.environ.get("K_SS", "99"))
BX = int(os.environ.get("K_BX", "0"))


def build_program():
    nc = bass.Bass("TRN2", target_bir_lowering=False)

    def din(name, shape):
        return nc.dram_tensor(name, shape, F32, kind="ExternalInput").ap()

    def dout(name, shape):
        return nc.dram_tensor(name, shape, F32, kind="ExternalOutput").ap()

    def dint(name, shape, dt):
        return nc.dram_tensor(name, shape, dt, kind="Internal").ap()

    xp = din("xp", [4, 2048, 1024]); xs = din("xs", [128, 1024]); memp = din("memp", [4, 256, 1024])
    cak = din("cak", [4, 512, 512]); cav = din("cav", [4, 512, 512])
    cbk = din("cbk", [4, 2048, 512]); cbv = din("cbv", [4, 2048, 512]); cbf = din("cbf", [4, 2048, 8])
    cck = din("cck", [4, 2048, 1024]); ccv = din("ccv", [4, 2048, 1024])
    cmk = din("cmk", [2, 4, 256, 1024]); cmv = din("cmv", [2, 4, 256, 1024])
    g_norm = din("g_norm", [64, 128])
    w_abin = din("w_abin", [1024, 3080]); b_f = din("b_f", [8]); g_ab = din("g_ab", [256]); relb = din("relb", [257, 8])
    w_about = din("w_about", [1024, 1024]); w_sbin = din("w_sbin", [1024, 3072]); w_sbout = din("w_sbout", [1024, 1024])
    w_xq = din("w_xq", [2, 1024, 1024]); w_xkv = din("w_xkv", [2, 1024, 2048]); g_x = din("g_x", [1024])
    w_xo = din("w_xo", [2, 1024, 1024]); w_up = din("w_up", [2, 1024, 4096]); w_dn = din("w_dn", [2, 4096, 1024])

    y_p = dout("y_p", [4, 2048, 1024]); y_s = dout("y_s", [128, 1024])
    akp = dout("akp", [4, 512, 512]); avp = dout("avp", [4, 512, 512]); aks = dout("aks", [128, 512]); avs = dout("avs", [128, 512])
    bkp = dout("bkp", [4, 2048, 512]); bvp = dout("bvp", [4, 2048, 512]); bfp = dout("bfp", [4, 2048, 8])
    bks = dout("bks", [128, 512]); bvs = dout("bvs", [128, 512]); bfs = dout("bfs", [128, 8])
    ckp = dout("ckp", [4, 2048, 1024]); cvp = dout("cvp", [4, 2048, 1024]); cks = dout("cks", [128, 1024]); cvs = dout("cvs", [128, 1024])
    mkp = dout("mkp", [2, 4, 256, 1024]); mvp = dout("mvp", [2, 4, 256, 1024])

    wbs_ab = dint("wbs_ab", [8, 128, 8 * 392], BF16)
    wbs_sb = dint("wbs_sb", [8, 128, 8 * 384], BF16)
    wb_about = dint("wb_about", [1024, 1024], BF16); wb_sbout = dint("wb_sbout", [1024, 1024], BF16)
    wb_xq = dint("wb_xq", [2, 1024, 1024], BF16); wb_xkv = dint("wb_xkv", [2, 1024, 2048], BF16)
    wb_xo = dint("wb_xo", [2, 1024, 1024], BF16); wb_up = dint("wb_up", [2, 1024, 4096], BF16); wb_dn = dint("wb_dn", [2, 4096, 1024], BF16)
    rev_d = dint("rev_d", [8, 768], F32)
    hk_d = dint("hk_d", [8, 128, 640], BF16)

    S = Sched()
    PHASE = 'PHASE'
    S.base_ctx = [PHASE]
    es = ExitStack()
    with es:
        ARENA_BYTES = 211000 // 64 * 64
        arena_t = es.enter_context(nc.sbuf_tensor("arena", [128, ARENA_BYTES // 2], BF16))
        A = Arena(arena_t, ARENA_BYTES)
        banks = [es.enter_context(nc.psum_tensor("bank%d" % i, [128, 512], F32)) for i in range(8)]

        def bank(i):
            return banks[i][:, :]

        def bank_bf(i):
            return banks[i][:, :].bitcast(BF16)

        def PE(fn, r, w):
            return S.add('pe', fn, r, w)

        def ACT(fn, r, w):
            return S.add('act', fn, r, w)

        def DVE(fn, r, w):
            return S.add('dve', fn, r, w)

        def POOL(fn, r, w):
            return S.add('pool', fn, r, w)

        def DMA(q, out, in_, r, w):
            return S.add(q, lambda e: e.dma_start(out=out, in_=in_), r, w, dma=True)

        def mm(out, lhsT, rhs, start, stop, r, w, skip=False):
            if skip:
                return PE(lambda e: e.matmul(out, lhsT=lhsT, rhs=rhs, start=start, stop=stop, skip_group_check=True), r, w)
            return PE(lambda e: e.matmul(out, lhsT=lhsT, rhs=rhs, start=start, stop=stop), r, w)

        def tr(out, in_, ident, r, w):
            return PE(lambda e: e.transpose(out=out, in_=in_, identity=ident), r, w)

        def act(out, in_, func, r, w, bias=None, scale=None, accum=None):
            kw = {}
            if bias is not None:
                kw['bias'] = bias
            if scale is not None:
                kw['scale'] = scale
            if accum is not None:
                kw['accum_out'] = accum
            return ACT(lambda e: e.activation(out=out, in_=in_, func=func, **kw), r, w)

        def tt(eng, out, in0, in1, op, r, w):
            return S.add(eng, lambda e: e.tensor_tensor(out=out, in0=in0, in1=in1, op=op), r, w)

        def ts(eng, out, in0, s1, op0, r, w, s2=None, op1=None):
            if op1 is None:
                return S.add(eng, lambda e: e.tensor_scalar(out=out, in0=in0, scalar1=s1, scalar2=None, op0=op0), r, w)
            return S.add(eng, lambda e: e.tensor_scalar(out=out, in0=in0, scalar1=s1, scalar2=s2, op0=op0, op1=op1), r, w)

        def cp(eng, out, in_, r, w):
            if eng == 'act':
                return ACT(lambda e: e.copy(out=out, in_=in_), r, w)
            return S.add(eng, lambda e: e.tensor_copy(out=out, in_=in_), r, w)

        def memset(eng, ap, v, r, w):
            return S.add(eng, lambda e: e.memset(ap, v), r, w)

        def asel(out, in_, pattern, cmp, fill, base, cm, r, w):
            return POOL(lambda e: e.affine_select(out=out, in_=in_, pattern=pattern, compare_op=cmp, fill=fill,
                                                  base=base, channel_multiplier=cm), r, w)

        identf = A.alloc([128, 128], F32); onesf = A.alloc([128, 128], F32); triUf = A.alloc([128, 128], F32)
        ident = A.alloc([128, 128], BF16); Jbf = A.alloc([128, 128], BF16); nident = A.alloc([128, 128], BF16)
        mask_le = A.alloc([128, 128], BF16); mask_lt = A.alloc([128, 128], BF16)
        ntriL = A.alloc([128, 128], BF16); nones = A.alloc([128, 128], BF16); ones_bf = A.alloc([128, 128], BF16)
        epst = A.alloc([128, 1], F32)
        gnT = A.alloc([128, 64], F32)
        gab = A.alloc([128, 4, 64], F32)
        gx = A.alloc([128, 2, 256], F32)
        bft = A.alloc([128, 8], F32)
        base_off = A.off
        global _BASE_OFF
        _BASE_OFF = base_off
        A.off = base_off + 65536
        tmpf = A.alloc([128, 128], F32)

        memset('pool', onesf, 1.0, [], ['onesf'])
        memset('pool', identf, 0.0, [], ['identf'])
        asel(identf, identf, [[-1, 128]], ALU.not_equal, 1.0, 0, 1, ['identf'], ['identf'])
        cp('pool', ident, identf, ['identf'], ['ident'])
        ts('pool', nident, identf, -1.0, ALU.mult, ['identf'], ['nident'])
        memset('pool', tmpf, 0.0, [], ['tmpf'])
        asel(tmpf, tmpf, [[1, 128]], ALU.not_equal, 1.0, -127, 1, ['tmpf'], ['tmpf'])
        cp('pool', Jbf, tmpf, ['tmpf'], ['Jbf'])
        memset('pool', tmpf, 0.0, ['tmpf'], ['tmpf'])
        asel(tmpf, tmpf, [[1, 128]], ALU.is_ge, NEGM, 0, -1, ['tmpf'], ['tmpf'])
        cp('pool', mask_le, tmpf, ['tmpf'], ['mask_le'])
        memset('pool', tmpf, 0.0, ['tmpf'], ['tmpf'])
        asel(tmpf, tmpf, [[1, 128]], ALU.is_gt, NEGM, 0, -1, ['tmpf'], ['tmpf'])
        cp('pool', mask_lt, tmpf, ['tmpf'], ['mask_lt'])
        memset('pool', tmpf, -1.0, ['tmpf'], ['tmpf'])
        asel(tmpf, tmpf, [[-1, 128]], ALU.is_ge, 0.0, 0, 1, ['tmpf'], ['tmpf'])
        cp('pool', ntriL, tmpf, ['tmpf'], ['ntriL'])
        memset('pool', nones, -1.0, [], ['nones'])
        memset('pool', ones_bf, 1.0, [], ['ones_bf'])
        memset('pool', triUf, 1.0, [], ['triUf'])
        asel(triUf, triUf, [[1, 128]], ALU.is_ge, 0.0, 0, -1, ['triUf'], ['triUf'])
        memset('pool', epst, EPS, [], ['epst'])

        S.mute = SETUP < 2
        g64 = A.alloc([128, 128], F32)
        DMA('sp', g64[0:64, :], g_norm, [], ['g64'])
        mm(bank(0)[:, 0:64], g64[0:64, :], identf[0:64, 0:64], True, True, ['g64', 'identf'], [('ps', 0)])
        cp('dve', gnT, bank(0)[:, 0:64], [('ps', 0)], ['gnT'])
        DMA('sp', gab.rearrange("p a b -> p (a b)"), bass.AP(tensor=g_ab.tensor, offset=0, ap=[[0, 128], [1, 256]]), [], ['gab'])
        DMA('sp', bft, bass.AP(tensor=b_f.tensor, offset=0, ap=[[0, 128], [1, 8]]), [], ['bft'])
        for j in (0, 2):
            ts('pool', gab[:, j, :], gab[:, j, :], 0.125, ALU.mult, ['gab'], ['gab'])

        S.mute = SETUP < 3
        def conv_nat(dst2d, src2d, key):
            R, N = src2d.shape
            a = R // 128
            d = dst2d.rearrange("(p a) n -> p (a n)", p=128)
            s = src2d.rearrange("(p a) n -> p (a n)", p=128)
            tot = a * N
            step = 8192
            for o in range(0, tot, step):
                DMA('pool', d[:, o:min(o + step, tot)], s[:, o:min(o + step, tot)], [], [])

        def conv_nat_k(dst2d, src2d, key):
            R, N = src2d.shape
            a = R // 128
            d = dst2d.rearrange("(p a) n -> p (a n)", p=128)
            s = src2d.rearrange("(p a) n -> p (a n)", p=128)
            tot = a * N
            step = 8192
            keys = []
            for idx, o in enumerate(range(0, tot, step)):
                k = ('wb', key, idx)
                DMA('pool', d[:, o:min(o + step, tot)], s[:, o:min(o + step, tot)], [], [k])
                keys.append(k)
            return keys

        WK = {}
        srcv = w_abin.rearrange("(c p) n -> p c n", p=128)
        kk = []
        for pi in range(8):
            base = (pi * 128) if pi < 4 else (1536 + (pi - 4) * 128)
            dstv = wbs_ab[pi].rearrange("p (c n) -> p c n", c=8)
            for j in range(3):
                col = base + j * 512
                k = ('wb', 'ab', pi, j)
                DMA('pool', dstv[:, :, j * 128:(j + 1) * 128], srcv[:, :, col:col + 128], [], [k])
                kk.append(k)
            if True:
                k = ('wb', 'ab', pi, 3)
                S.add('pool', lambda e, o=dstv[:, :, 384:392], i=srcv[:, :, 3072:3080]: e.dma_start(out=o, in_=i), [], [k], dma=True)
                kk.append(k)
        WK['ab'] = kk
        WK['about'] = conv_nat_k(wb_about, w_about, 'about')
        for l in range(2):
            WK[('xkv', l)] = conv_nat_k(wb_xkv[l], w_xkv[l], ('xkv', l))
            WK[('xq', l)] = conv_nat_k(wb_xq[l], w_xq[l], ('xq', l))
            WK[('xo', l)] = conv_nat_k(wb_xo[l], w_xo[l], ('xo', l))
            WK[('up', l)] = conv_nat_k(wb_up[l], w_up[l], ('up', l))
            WK[('dn', l)] = conv_nat_k(wb_dn[l], w_dn[l], ('dn', l))
            if l == 0:
                srcv2 = w_sbin.rearrange("(c p) n -> p c n", p=128)
                kk = []
                for pi in range(8):
                    dstv = wbs_sb[pi].rearrange("p (c n) -> p c n", c=8)
                    for j in range(3):
                        col = pi * 128 + j * 1024
                        k = ('wb', 'sb', pi, j)
                        DMA('pool', dstv[:, :, j * 128:(j + 1) * 128], srcv2[:, :, col:col + 128], [], [k])
                        kk.append(k)
                WK['sb'] = kk
                WK['sbout'] = conv_nat_k(wb_sbout, w_sbout, 'sbout')

        S.mute = SETUP < 4
        rbt = A.alloc([128, 2, 8], F32)
        rbb = A.alloc([128, 2, 8], BF16)
        e0 = A.alloc([128, 512], BF16)
        revsb = A.alloc([128, 768], F32)
        DMA('sp', rbt, relb[0:256, :].rearrange("(a p) h -> p a h", p=128), [], ['rbt'])
        cp('dve', rbb, rbt, ['rbt'], ['rbb'])
        memset('pool', e0, 0.0, [], ['e0'])
        memset('pool', e0[0:1, :], 1.0, ['e0'], ['e0'])
        mm(bank(1)[0:8, 0:128], rbb[:, 1, :], Jbf, True, True, ['rbb', 'Jbf'], [('ps', 1)])
        mm(bank(1)[0:8, 128:256], rbb[:, 0, :], Jbf, True, True, ['rbb', 'Jbf'], [('ps', 1)])
        mm(bank(2)[0:8, 0:512], rbb[:, 0, :], e0, True, True, ['rbb', 'e0'], [('ps', 2)])
        cp('dve', revsb[0:8, 0:256], bank(1)[0:8, 0:256], [('ps', 1)], ['revsb'])
        cp('dve', revsb[0:8, 256:768], bank(2)[0:8, 0:512], [('ps', 2)], ['revsb'])
        DMA('sp', rev_d, revsb[0:8, :], ['revsb'], ['rev_d'])

        SETUP_KEYS = ['tmpf', 'g64', 'rbt', 'rbb', 'e0', 'revsb']

        class U:
            pass

        def norm_unit(u, kind, l, mm_banks):
            gcol = kind * 16 + l * 8
            for t in range(u.NT):
                xb, xk = u.xnb.next()
                st, sk = u.small.next()
                act(xb, u.h[:, t, :], AF.Square, [('h', u.n, t)], [xk, sk], accum=st[:, 0:1])
                act(st[:, 1:2], st[:, 0:1], AF.Ln, [sk, 'epst'], [sk], bias=epst[:, 0:1], scale=1.0 / 1024.0)
                act(st[:, 2:3], st[:, 1:2], AF.Exp, [sk], [sk], scale=-0.5)
                ts('dve', xb, u.h[:, t, :], st[:, 2:3], ALU.mult, [('h', u.n, t), sk, xk], [xk])
                bk = u.trb.next()
                pv = bank_bf(bk).rearrange("p (c n) -> p c n", c=8)
                for c in range(8):
                    tr(pv[:, c, :], xb[:, c * 128:(c + 1) * 128], ident, [xk, 'ident'], [('ps', bk)])
                tt('dve', u.xnT[:, :, t * 128:(t + 1) * 128], pv,
                   gnT[:, gcol:gcol + 8].unsqueeze(2).to_broadcast([128, 8, 128]), ALU.mult,
                   [('ps', bk), 'gnT'], [('xnT', u.n, t)])

        ring = {}

        def ring_setup(ar, nslots):
            ring['slots'] = [ar.alloc([128, 4096], BF16) for _ in range(nslots)]
            ring['i'] = 0

        def ring_load(srcs, deps):
            i = ring['i'] % len(ring['slots'])
            ring['i'] += 1
            slot = ring['slots'][i]
            key = ('ring', i)
            for (dst_fn, src) in srcs:
                DMA('sp', dst_fn(slot), src, deps, [key])
            return slot, key

        def headnorm_rstd(u, pp, ncols, ngrp, gw, rkeys):
            sq, sqk = u.sq.next()
            st, sk = u.small.next()
            act(sq[:, 0:ncols], pp[:, 0:ncols], AF.Square, rkeys, [sqk])
            S.add('dve', lambda e: e.tensor_reduce(out=st[:, 0:ngrp], in_=sq[:, 0:ncols].rearrange("p (g w) -> p g w", g=ngrp),
                                                   axis=AX.X, op=ALU.add), [sqk], [sk])
            act(st[:, 8:8 + ngrp], st[:, 0:ngrp], AF.Ln, [sk, 'epst'], [sk], bias=epst[:, 0:1], scale=1.0 / gw)
            act(st[:, 16:16 + ngrp], st[:, 8:8 + ngrp], AF.Exp, [sk], [sk], scale=-0.5)
            return st[:, 16:16 + ngrp], sk

        def proj_pair(u, kind, pi, out_k, out_v, out_f, mm_banks):
            isab = kind == 'ab'
            hasf = isab and pi == 4
            N = 392 if hasf else 384
            wsrc = wbs_ab[pi] if isab else wbs_sb[pi]
            wcols = 392 if isab else 384
            wkeys = WK['ab'] if isab else WK['sb']
            slot, rk = ring_load([(lambda s: s[:, 0:8 * wcols], wsrc)], wkeys)
            W = slot[:, 0:8 * wcols].rearrange("p (c n) -> p c n", c=8)
            gq = (0 if pi < 4 else 2)
            for t in range(u.NT):
                S.mute = False
                bk = mm_banks.next()
                pp = bank(bk)
                for c in range(8):
                    mm(pp[:, 0:N], u.xnT[:, c, t * 128:(t + 1) * 128], W[:, c, 0:N], c == 0, c == 7,
                       [('xnT', u.n, t), rk], [('ps', bk)])
                S.mute = PPL < 1
                tg = t % 2
                if tg == 0:
                    kst, kstk = u.kst.next()
                    vst, vstk = u.vst.next()
                    u.cur_st = (kst, kstk, vst, vstk)
                kst, kstk, vst, vstk = u.cur_st
                qkb, qkbk = u.qkb.next()
                if isab:
                    rstd, sk = headnorm_rstd(u, pp, 256, 4, 64, [('ps', bk)])
                    qn, qnk = u.qkn.next()
                    tt('dve', qn[:, 0:256].rearrange("p (g w) -> p g w", g=4), pp[:, 0:256].rearrange("p (g w) -> p g w", g=4),
                       rstd.unsqueeze(2).to_broadcast([128, 4, 64]), ALU.mult, [('ps', bk), sk], [qnk])
                    tt('dve', qkb[:, 0:128].rearrange("p (g w) -> p g w", g=2), qn[:, 0:128].rearrange("p (g w) -> p g w", g=2),
                       gab[:, gq:gq + 1, :].to_broadcast([128, 2, 64]), ALU.mult, [qnk, 'gab'], [qkbk])
                    tt('dve', kst[:, tg, :].rearrange("p (g w) -> p g w", g=2), qn[:, 128:256].rearrange("p (g w) -> p g w", g=2),
                       gab[:, gq + 1:gq + 2, :].to_broadcast([128, 2, 64]), ALU.mult, [qnk, 'gab'], [kstk])
                    cp('act', qkb[:, 128:256], kst[:, tg, :], [kstk], [qkbk])
                else:
                    act(qkb[:, 0:128], pp[:, 0:128], AF.Copy, [('ps', bk)], [qkbk], scale=0.125)
                    cp('act', kst[:, tg, :], pp[:, 128:256], [('ps', bk)], [kstk])
                    cp('act', qkb[:, 128:256], kst[:, tg, :], [kstk], [qkbk])
                S.mute = PPL < 2
                cp('act', vst[:, tg, :], pp[:, 256:384], [('ps', bk)], [vstk])
                cp('dve', u.Vaug[:, u.voff + t, :, 0:64], pp[:, 256:384].rearrange("p (g w) -> p g w", g=2),
                   [('ps', bk)], [('vaug', u.voff + t)])
                S.mute = PPL < 3
                if hasf:
                    st, sk2 = u.small.next()
                    tt('dve', st[:, 0:8], pp[:, 384:392], bft, ALU.add, [('ps', bk), 'bft'], [sk2])
                    act(st[:, 8:16], st[:, 0:8], AF.Exp, [sk2], [sk2], scale=-1.0)
                    act(st[:, 16:24], st[:, 8:16], AF.Ln, [sk2], [sk2], bias=1.0)
                    ts('dve', u.lf[:, t, :], st[:, 16:24], -1.0, ALU.mult, [sk2], [('lf', t)])
                S.mute = PPL < 4
                tb = u.trb.next()
                pv = bank_bf(tb).rearrange("p (c n) -> p c n", c=8)
                tr(pv[:, 0, :], qkb[:, 0:128], ident, [qkbk, 'ident'], [('ps', tb)])
                tr(pv[:, 1, :], qkb[:, 128:256], ident, [qkbk, 'ident'], [('ps', tb)])
                cp('dve', u.QT[:, t * 128:(t + 1) * 128], pv[:, 0, :], [('ps', tb)], [('qt', t)])
                cp('dve', u.KT[:, u.koff + t * 128:u.koff + (t + 1) * 128], pv[:, 1, :], [('ps', tb)], [('kt', u.koff // 128 + t)])
                S.mute = PPL < 5
                if tg == 1 or t == u.NT - 1:
                    g0 = t - tg
                    ntt = tg + 1
                    ok = out_k(g0, ntt)
                    if ok is not None:
                        DMA('pool', ok, kst[:, 0:ntt, :], [kstk], [])
                    ov = out_v(g0, ntt)
                    if ov is not None:
                        DMA('pool', ov, vst[:, 0:ntt, :], [vstk], [])
            S.mute = PPL < 6
            if hasf and out_f is not None:
                DMA('pool', out_f, u.lf[:, 0:u.NT, :], [('lf', t) for t in range(u.NT)], [])
            S.mute = False

        def out_proj(u, wb2d, wkeys, mm_banks):
            wv = wb2d.rearrange("(c p) n -> p c n", p=128)
            slots = []
            for half in range(2):
                slot, rk = ring_load([(lambda s: s[:, 0:4096].rearrange("p (c n) -> p c n", c=8), wv[:, :, half * 512:(half + 1) * 512])], wkeys)
                slots.append((slot[:, 0:4096].rearrange("p (c n) -> p c n", c=8), rk))
            for t in range(u.NT):
                tb = u.trb.next()
                pv = bank_bf(tb).rearrange("p (c n) -> p c n", c=8)
                for c in range(8):
                    tr(pv[:, c, :], u.otok[:, t, c * 128:(c + 1) * 128], ident, [('ot', t), 'ident'], [('ps', tb)])
                ot, otk = u.ott.next()
                cp('dve', ot, pv, [('ps', tb)], [otk])
                for half in range(2):
                    Wh, rk = slots[half]
                    bk = mm_banks.next()
                    pp = bank(bk)
                    for c in range(8):
                        mm(pp, ot[:, c, :], Wh[:, c, :], c == 0, c == 7, [otk, rk], [('ps', bk)])
                    hs = u.h[:, t, half * 512:(half + 1) * 512]
                    tt('dve', hs, hs, pp, ALU.add, [('ps', bk), ('h', u.n, t)], [('h', u.n, t)])

        def mlp(u, l, mm_banks):
            upv = wb_up[l].rearrange("(c p) n -> p c n", p=128)
            dnv = wb_dn[l].rearrange("(g c p) n -> g p c n", p=128, c=4)
            TC = 512 if u.NT >= 4 else u.NT * 128
            ntc = u.NT * 128 // TC
            for g in range(8):
                slot_u, rku = ring_load([(lambda s: s[:, 0:4096].rearrange("p (c n) -> p c n", c=8), upv[:, :, g * 512:(g + 1) * 512])], WK[('up', l)])
                Wu = slot_u[:, 0:4096].rearrange("p (c n) -> p c n", c=8)
                slot_d, rkd = ring_load([(lambda s: s[:, 0:4096].rearrange("p (c n) -> p c n", c=4), dnv[g])], WK[('dn', l)])
                Wd = slot_d[:, 0:4096].rearrange("p (c n) -> p c n", c=4)
                for f in range(4):
                    for tc in range(ntc):
                        bk = mm_banks.next()
                        pp = bank(bk)
                        for c in range(8):
                            mm(pp[:, 0:TC], Wu[:, c, f * 128:(f + 1) * 128], u.xnT[:, c, tc * TC:(tc + 1) * TC], c == 0, c == 7,
                               [rku] + [('xnT', u.n, tc * (TC // 128) + j) for j in range(TC // 128)], [('ps', bk)])
                        hk_ = [('ot', 2 * f), ('ot', 2 * f + 1)] if u.NT > 1 else [('ot', 0)]
                        hs_ = u.h1[:, f, tc * TC:(tc + 1) * TC]
                        act(hs_, pp[:, 0:TC], AF.Relu, [('ps', bk)], hk_)
                        tt('pool', hs_, hs_, hs_, ALU.mult, hk_, hk_)
                for t in range(u.NT):
                    for half in range(2):
                        bk = mm_banks.next()
                        pp = bank(bk)
                        for f in range(4):
                            mm(pp, u.h1[:, f, t * 128:(t + 1) * 128], Wd[:, f, half * 512:(half + 1) * 512], f == 0, f == 3,
                               [rkd] + ([('ot', 2 * f), ('ot', 2 * f + 1)] if u.NT > 1 else [('ot', 0)]), [('ps', bk)])
                        hs = u.h[:, t, half * 512:(half + 1) * 512]
                        tt('dve', hs, hs, pp, ALU.add, [('ps', bk), ('h', u.n, t)], [('h', u.n, t)])

        def attn_A_prompt(u, p, zb, ob):
            hk, hkk = u.hk.next()
            DMA('sp', hk, hk_d[2 * p:2 * p + 2].rearrange("h p n -> p h n"), ['hk_d'], [hkk])
            for j in range(u.NT):
                ds = [d for d in range(5) if j - d >= 0]
                obk = ob.next()
                O = bank(obk)[:, 0:130].rearrange("p (g w) -> p g w", g=2)
                for di, d in enumerate(ds):
                    i = j - d
                    zk = zb.next()
                    Z = bank(zk)[:, 0:256].rearrange("p (g w) -> p g w", g=2)
                    for hh in range(2):
                        ps_ = slice(hh * 64, (hh + 1) * 64)
                        mm(Z[:, hh, :], u.KT[ps_, i * 128:(i + 1) * 128], u.QT[ps_, j * 128:(j + 1) * 128], True, False,
                           [('kt', i), ('qt', j)], [('ps', zk)])
                        mm(Z[:, hh, :], Jbf, hk[:, hh, d * 128:(d + 1) * 128], False, True, ['Jbf', hkk], [('ps', zk)])
                    pt, ptk = u.pt.next()
                    P = pt[:, 0:256].rearrange("p (g w) -> p g w", g=2)
                    act(P, Z, AF.Exp, [('ps', zk)], [ptk])
                    for hh in range(2):
                        mm(O[:, hh, :], P[:, hh, :], u.Vaug[:, i, hh, :], di == 0 and hh == 0, di == len(ds) - 1,
                           [ptk, ('vaug', i)], [('ps', obk)], skip=True)
                st, sk = u.small.next()
                S.add('dve', lambda e, o=st[:, 0:2], i_=O[:, :, 64]: e.reciprocal(out=o, in_=i_), [('ps', obk)], [sk])
                tt('dve', u.otok[:, j, p * 128:(p + 1) * 128].rearrange("p (g w) -> p g w", g=2), O[:, :, 0:64],
                   st[:, 0:2].unsqueeze(2).to_broadcast([128, 2, 64]), ALU.mult, [('ps', obk), sk], [('ot', j)])

        def fox_cumsum(u):
            NT = u.NT
            memset('dve', u.pref[:, 0, :], 0.0, [], [('pref', 0)])
            for T in range(NT):
                tt('dve', u.pref[:, T + 1, :], u.pref[:, T, :], u.lf[:, T, :], ALU.add, [('pref', T), ('lf', T)], [('pref', T + 1)])
            for T in range(NT):
                bk = u.mmb.next()
                pp = bank(bk)
                mm(pp[:, 0:8], triUf, u.lf[:, T, :], True, False, ['triUf', ('lf', T)], [('ps', bk)])
                mm(pp[:, 0:8], onesf, u.pref[:, T, :], False, True, ['onesf', ('pref', T)], [('ps', bk)])
                mm(pp[:, 8:16], onesf, u.pref[:, T + 1, :], True, True, ['onesf', ('pref', T + 1)], [('ps', bk)])
                cp('dve', u.cT[:, T, :], pp[:, 0:8], [('ps', bk)], [('cT', T)])
                cp('dve', u.car[:, T, :], pp[:, 8:16], [('ps', bk)], [('car', T)])

        def attn_B_prompt(u, p, zb):
            NJ = u.NT // 4
            for J in range(NJ):
                nk = 4 * J + 4
                if p == 0:
                    for i in range(nk):
                        tt('dve', u.bcol[:, J, i, :], u.car[:, 4 * J + 3, :], u.cT[:, i, :], ALU.subtract,
                           [('car', 4 * J + 3), ('cT', i)], [('bcol', J, i)])
                for hh in range(2):
                    hidx = 2 * p + hh
                    ps_ = slice(hh * 64, (hh + 1) * 64)
                    obk = 6 + hh
                    O = bank(obk)[:, 0:260].rearrange("p (g w) -> p g w", g=4)
                    for i in range(nk):
                        e_ = max(0, i - 4 * J)
                        c0 = e_ * 128
                        diag = i >= 4 * J
                        zk = zb.next()
                        Z = bank(zk)
                        q0 = J * 512
                        if diag:
                            mm(Z[:, c0:c0 + 128], u.KT[ps_, i * 128:(i + 1) * 128], u.QT[ps_, q0 + c0:q0 + c0 + 128], True, False,
                               [('kt', i), ('qt', 4 * J + e_)], [('ps', zk)])
                            mm(Z[:, c0:c0 + 128], ident, mask_le, False, True, ['ident', 'mask_le'], [('ps', zk)])
                            if c0 + 128 < 512:
                                mm(Z[:, c0 + 128:512], u.KT[ps_, i * 128:(i + 1) * 128], u.QT[ps_, q0 + c0 + 128:q0 + 512], True, True,
                                   [('kt', i)] + [('qt', 4 * J + x) for x in range(e_ + 1, 4)], [('ps', zk)])
                        else:
                            mm(Z, u.KT[ps_, i * 128:(i + 1) * 128], u.QT[ps_, q0:q0 + 512], True, True,
                               [('kt', i)] + [('qt', 4 * J + x) for x in range(4)], [('ps', zk)])
                        pt, ptk = u.pt.next()
                        act(pt[:, c0:512], Z[:, c0:512], AF.Exp, [('ps', zk), ('bcol', J, i)], [ptk],
                            bias=u.bcol[:, J, i, hidx:hidx + 1])
                        for e2 in range(e_, 4):
                            mm(O[:, e2, :], pt[:, e2 * 128:(e2 + 1) * 128], u.Vaug[:, i, hh, :], i == 0 and e2 == 0, i == 4 * J + e2,
                               [ptk, ('vaug', i), ('ps', obk)], [('ps', obk, e2)], skip=True)
                    st, sk = u.small.next()
                    S.add('dve', lambda e, o=st[:, 0:4], i_=O[:, :, 64]: e.reciprocal(out=o, in_=i_),
                          [('ps', obk, x) for x in range(4)] + [('ps', obk)], [sk])
                    fo = 512 + hidx * 64
                    tt('dve', u.otok[:, 4 * J:4 * J + 4, fo:fo + 64], O[:, :, 0:64],
                       st[:, 0:4].unsqueeze(2).to_broadcast([128, 4, 64]), ALU.mult,
                       [('ps', obk, x) for x in range(4)] + [sk, ('ps', obk)], [('ot', 4 * J + x) for x in range(4)])

        def attn_C_prompt(u, p, zb):
            NJ = u.NT // 4
            for J in range(NJ):
                nk = 4 * J + 4
                q0 = J * 512
                for i in range(nk - 1, -1, -1):
                    for hh in range(2):
                        ps_ = slice(hh * 64, (hh + 1) * 64)
                        e_ = max(0, i - 4 * J)
                        c0 = e_ * 128
                        diag = i >= 4 * J
                        first = (i == nk - 1)
                        O = bank(6 + hh)[:, 0:256].rearrange("p (g w) -> p g w", g=4)
                        E, Ek = u.sbE[hh]; L, Lk = u.sbL[hh]; Sf, Sfk = u.sbSf[hh]; Sb, Sbk = u.sbSb[hh]; Wt, Wtk = u.sbW[hh]
                        zk = zb.next()
                        Z = bank(zk)
                        kq = [('kt', i)] + [('qt', 4 * J + x) for x in range(e_, 4)]
                        if diag:
                            mm(Z[:, c0:c0 + 128], u.KT[ps_, i * 128:(i + 1) * 128], u.QT[ps_, q0 + c0:q0 + c0 + 128], True, False, kq, [('ps', zk)])
                            mm(Z[:, c0:c0 + 128], ident, mask_lt, False, True, ['ident', 'mask_lt'], [('ps', zk)])
                            if c0 + 128 < 512:
                                mm(Z[:, c0 + 128:512], u.KT[ps_, i * 128:(i + 1) * 128], u.QT[ps_, q0 + c0 + 128:q0 + 512], True, True, kq, [('ps', zk)])
                        else:
                            mm(Z, u.KT[ps_, i * 128:(i + 1) * 128], u.QT[ps_, q0:q0 + 512], True, True, kq, [('ps', zk)])
                        act(E[:, c0:512], Z[:, c0:512], AF.Exp, [('ps', zk)], [Ek])
                        act(L[:, c0:512], E[:, c0:512], AF.Ln, [Ek], [Lk], bias=1.0)
                        rk_ = zb.next()
                        R = bank(rk_)
                        if diag:
                            mm(R[:, c0:c0 + 128], u.KT[ps_, i * 128:(i + 1) * 128], u.QT[ps_, q0 + c0:q0 + c0 + 128], True, False, kq, [('ps', rk_)])
                            mm(R[:, c0:c0 + 128], ident, mask_lt, False, False, ['ident', 'mask_lt'], [('ps', rk_)])
                            mm(R[:, c0:c0 + 128], ntriL, L[:, c0:c0 + 128], False, True, ['ntriL', Lk], [('ps', rk_)])
                            if c0 + 128 < 512:
                                mm(R[:, c0 + 128:512], u.KT[ps_, i * 128:(i + 1) * 128], u.QT[ps_, q0 + c0 + 128:q0 + 512], True, False, kq, [('ps', rk_)])
                                mm(R[:, c0 + 128:512], ntriL, L[:, c0 + 128:512], False, False, ['ntriL', Lk], [('ps', rk_)])
                                mm(R[:, c0 + 128:512], nones, Sb[:, c0 + 128:512], False, True, ['nones', Sbk], [('ps', rk_)])
                        else:
                            mm(R, u.KT[ps_, i * 128:(i + 1) * 128], u.QT[ps_, q0:q0 + 512], True, False, kq, [('ps', rk_)])
                            mm(R, ntriL, L, False, False, ['ntriL', Lk], [('ps', rk_)])
                            mm(R, nones, Sb, False, True, ['nones', Sbk], [('ps', rk_)])
                        act(Wt[:, c0:512], R[:, c0:512], AF.Exp, [('ps', rk_)], [Wtk])
                        for e2 in range(e_, 4):
                            mm(O[:, e2, :], Wt[:, e2 * 128:(e2 + 1) * 128], u.Vaug[:, i, hh, 0:64], i == nk - 1 and e2 == 3, i == 0,
                               [Wtk, ('vaug', i), ('ps', 6 + hh)], [('ps', 6 + hh, e2)], skip=True)
                        if i > 0:
                            if diag:
                                cp('pool', Sf[:, c0:c0 + 128], L[:, c0:c0 + 128], [Lk], [Sfk])
                                if c0 + 128 < 512:
                                    tt('pool', Sf[:, c0 + 128:512], Sf[:, c0 + 128:512], L[:, c0 + 128:512], ALU.add, [Lk, Sfk], [Sfk])
                            else:
                                tt('pool', Sf, Sf, L, ALU.add, [Lk, Sfk], [Sfk])
                            cp('pool', Sb[:, c0:512], Sf[:, c0:512], [Sfk], [Sbk])
                for hh in range(2):
                    O = bank(6 + hh)[:, 0:256].rearrange("p (g w) -> p g w", g=4)
                    fo = (2 * p + hh) * 64
                    cp('act', u.otok[:, 4 * J:4 * J + 4, fo:fo + 64], O, [('ps', 6 + hh, x) for x in range(4)] + [('ps', 6 + hh)],
                       [('ot', 4 * J + x) for x in range(4)])

        def mem_kv_prompt(u, l, b, mm_banks):
            gcol = 3 * 16 + l * 8
            MEMK = [('ot', x) for x in range(4)]
            XMK = [('ot', 4), ('ot', 5)]
            DMA('sp', u.memt, memp[b].rearrange("(a p) n -> p a n", p=128), [], MEMK)
            for mt in range(2):
                xb, xk = u.xnb.next()
                st, sk = u.small.next()
                act(xb, u.memt[:, mt, :], AF.Square, MEMK, [xk, sk], accum=st[:, 0:1])
                act(st[:, 1:2], st[:, 0:1], AF.Ln, [sk, 'epst'], [sk], bias=epst[:, 0:1], scale=1.0 / 1024.0)
                act(st[:, 2:3], st[:, 1:2], AF.Exp, [sk], [sk], scale=-0.5)
                ts('dve', xb, u.memt[:, mt, :], st[:, 2:3], ALU.mult, MEMK + [sk, xk], [xk])
                bk = u.trb.next()
                pv = bank_bf(bk).rearrange("p (c n) -> p c n", c=8)
                for c in range(8):
                    tr(pv[:, c, :], xb[:, c * 128:(c + 1) * 128], ident, [xk, 'ident'], [('ps', bk)])
                tt('dve', u.xnTm[:, :, mt * 128:(mt + 1) * 128], pv,
                   gnT[:, gcol:gcol + 8].unsqueeze(2).to_broadcast([128, 8, 128]), ALU.mult, [('ps', bk), 'gnT'] + XMK, XMK)
            wv = wb_xkv[l].rearrange("(c p) n -> p c n", p=128)
            S.mute = ML < 2
            for n in range(4 if ML >= 5 else 2):
                slot, rk = ring_load([(lambda s: s[:, 0:4096].rearrange("p (c n) -> p c n", c=8), wv[:, :, n * 512:(n + 1) * 512])], WK[('xkv', l)])
                W = slot[:, 0:4096].rearrange("p (c n) -> p c n", c=8)
                for mt in range(2):
                    bk = mm_banks.next()
                    pp = bank(bk)
                    for c in range(8):
                        mm(pp, u.xnTm[:, c, mt * 128:(mt + 1) * 128], W[:, c, :], c == 0, c == 7, XMK + [rk], [('ps', bk)])
                    stg2, stgk = u.qkn.next()
                    if n < 2:
                        rstd, sk = headnorm_rstd(u, pp, 512, 2, 256, [('ps', bk)])
                        qn, qnk = stg2, stgk
                        tt('dve', qn.rearrange("p (g w) -> p g w", g=2), pp.rearrange("p (g w) -> p g w", g=2),
                           rstd.unsqueeze(2).to_broadcast([128, 2, 256]), ALU.mult, [('ps', bk), sk], [qnk])
                        tt('dve', stg2.rearrange("p (g w) -> p g w", g=2), qn.rearrange("p (g w) -> p g w", g=2),
                           gx[:, 1:2, :].to_broadcast([128, 2, 256]), ALU.mult, [qnk, 'gx'], [stgk])
                        S.mute = ML < 3
                        DMA('pool', mkp[l, b, mt * 128:(mt + 1) * 128, n * 512:(n + 1) * 512], stg2, [stgk], [])
                        S.mute = ML < 4
                        qkb, qkbk = u.qkb.next()
                        cp('act', qkb, stg2, [stgk], [qkbk])
                        tb = u.trb.next()
                        pv = bank_bf(tb).rearrange("p (c n) -> p c n", c=8)
                        for c in range(4):
                            tr(pv[:, c, :], qkb[:, c * 128:(c + 1) * 128], ident, [qkbk, 'ident'], [('ps', tb)])
                        cp('dve', u.KmT[:, n * 4:n * 4 + 4, mt * 128:(mt + 1) * 128], pv[:, 0:4, :], [('ps', tb)], [('kmt', n, mt)])
                        S.mute = ML < 2
                    else:
                        cp('act', stg2, pp, [('ps', bk)], [stgk])
                        S.mute = ML < 6
                        DMA('pool', mvp[l, b, mt * 128:(mt + 1) * 128, (n - 2) * 512:(n - 1) * 512], stg2, [stgk], [])
                        S.mute = ML < 7
                        cp('act', u.Vm[:, mt, (n - 2) * 2:(n - 2) * 2 + 2, :].rearrange("p g w -> p (g w)"), pp, [('ps', bk)], [('vm', n, mt)])
                        S.mute = False

        def cross_q_chunk(u, l, tiles, Wq, mm_banks):
            for li, t in enumerate(tiles):
                for half in range(2):
                    Wh, rk = Wq[half]
                    bk = mm_banks.next()
                    pp = bank(bk)
                    for c in range(8):
                        mm(pp, u.xnT[:, c, t * 128:(t + 1) * 128], Wh[:, c, :], c == 0, c == 7, [('xnT', u.n, t), rk], [('ps', bk)])
                    rstd, sk = headnorm_rstd(u, pp, 512, 2, 256, [('ps', bk)])
                    qn, qnk = u.qkn.next()
                    tt('dve', qn.rearrange("p (g w) -> p g w", g=2), pp.rearrange("p (g w) -> p g w", g=2),
                       rstd.unsqueeze(2).to_broadcast([128, 2, 256]), ALU.mult, [('ps', bk), sk], [qnk])
                    qkb, qkbk = u.qkb.next()
                    tt('dve', qkb.rearrange("p (g w) -> p g w", g=2), qn.rearrange("p (g w) -> p g w", g=2),
                       gx[:, 0:1, :].to_broadcast([128, 2, 256]), ALU.mult, [qnk, 'gx'], [qkbk])
                    tb = u.trb.next()
                    pv = bank_bf(tb).rearrange("p (c n) -> p c n", c=8)
                    for c in range(4):
                        tr(pv[:, c, :], qkb[:, c * 128:(c + 1) * 128], ident, [qkbk, 'ident'], [('ps', tb)])
                    cp('dve', u.QxT[:, half * 4:half * 4 + 4, li * 128:(li + 1) * 128], pv[:, 0:4, :], [('ps', tb)], [('qxt', li)])

        def cross_attn_prompt(u, l, mm_banks, zb):
            wv = wb_xq[l].rearrange("(c p) n -> p c n", p=128)
            Wq = []
            for half in range(2):
                slot, rk = ring_load([(lambda s: s[:, 0:4096].rearrange("p (c n) -> p c n", c=8), wv[:, :, half * 512:(half + 1) * 512])], WK[('xq', l)])
                Wq.append((slot[:, 0:4096].rearrange("p (c n) -> p c n", c=8), rk))
            for J in range(u.NT // 4):
                cross_q_chunk(u, l, [4 * J + x for x in range(4)], Wq, mm_banks)
                for h in range(4):
                    pts = []
                    for mt in range(2):
                        zk = zb.next()
                        Z = bank(zk)
                        for dc in range(2):
                            mm(Z, u.KmT[:, h * 2 + dc, mt * 128:(mt + 1) * 128], u.QxT[:, h * 2 + dc, :], dc == 0, dc == 1,
                               [('kmt', h // 2, mt)] + [('qxt', x) for x in range(4)], [('ps', zk)])
                        pt, ptk = u.pt.next()
                        act(pt, Z, AF.Exp, [('ps', zk)], [ptk])
                        pts.append((pt, ptk))
                    for x in range(4):
                        obk = 6 + (x // 2)
                        O = bank(obk)[:, (x % 2) * 256:(x % 2 + 1) * 256]
                        for mt in range(2):
                            mm(O, pts[mt][0][:, x * 128:(x + 1) * 128], u.Vm[:, mt, h, :], mt == 0 and x % 2 == 0, mt == 1,
                               [pts[mt][1], ('vm', 2 + h // 2, mt), ('ps', obk)], [('ps', obk, x % 2)], skip=True)
                    rsb = 1
                    RS = bank(rsb)[:, 0:64].rearrange("p (x n) -> p x n", x=4)
                    for x in range(4):
                        for mt in range(2):
                            mm(RS[:, x, :], pts[mt][0][:, x * 128:(x + 1) * 128], ones_bf[:, 0:16], mt == 0 and x == 0, mt == 1,
                               [pts[mt][1], 'ones_bf'], [('ps', rsb)], skip=True)
                    st, sk = u.small.next()
                    S.add('dve', lambda e, o=st[:, 0:4], i_=RS[:, :, 0]: e.reciprocal(out=o, in_=i_), [('ps', rsb)], [sk])
                    for x in range(4):
                        obk = 6 + (x // 2)
                        O = bank(obk)[:, (x % 2) * 256:(x % 2 + 1) * 256]
                        ts('dve', u.otok[:, 4 * J + x, h * 256:(h + 1) * 256], O, st[:, x:x + 1], ALU.mult,
                           [('ps', obk, x % 2), sk, ('ps', obk)], [('ot', 4 * J + x)])

        P_ = U()
        P_.n = 'p'
        P_.NT = 16
        A.off = base_off
        P_.h = A.alloc([128, 16, 1024], F32)
        P_.xnT = A.alloc([128, 8, 2048], BF16)
        P_.otok = A.alloc([128, 16, 1024], BF16)
        P_.h1 = P_.otok.rearrange("p a b -> p (a b)")[:, 0:8192].rearrange("p (f n) -> p f n", f=4)
        ring_setup(A, 3)
        ra0 = A.off
        qkt = A.alloc([128, 2, 2048], BF16)
        P_.QT = qkt[:, 0, :]; P_.KT = qkt[:, 1, :]
        P_.Vaug = A.alloc([128, 16, 2, 65], BF16)
        ra1 = A.off
        A.off = ra0
        P_.KmT = A.alloc([128, 8, 256], BF16)
        P_.Vm = A.alloc([128, 2, 4, 256], BF16)
        A.off = max(A.off, ra1)
        P_.koff = 0; P_.voff = 0
        P_.hk = Rot([(A.alloc([128, 2, 640], BF16), ('hk', i)) for i in range(1)])
        P_.pt = Rot([(A.alloc([128, 512], BF16), ('pt', i)) for i in range(2)])
        un0 = A.off
        P_.bcol = A.alloc([128, 4, 16, 8], F32)
        P_.cT = A.alloc([128, 16, 8], F32); P_.car = A.alloc([128, 16, 8], F32); P_.pref = A.alloc([128, 17, 8], F32)
        P_.lf = A.alloc([128, 16, 8], F32)
        un_a = A.off
        A.off = un0
        P_.sbE = [(A.alloc([128, 512], BF16), ('sbL', i)) for i in range(2)]
        P_.sbL = P_.sbE
        P_.sbSf = [(A.alloc([128, 512], F32), ('sbSf', i)) for i in range(2)]
        P_.sbSb = [(A.alloc([128, 512], BF16), ('sbSb', i)) for i in range(2)]
        P_.sbW = [(A.alloc([128, 512], BF16), ('sbW', i)) for i in range(2)]
        un_b = A.off
        A.off = un0
        P_.QxT = A.alloc([128, 8, 512], BF16)
        un_c = A.off
        A.off = max(un_a, un_b, un_c)
        P_.kst = Rot([(A.alloc([128, 2, 128], F32), ('kst', i)) for i in range(2)])
        P_.vst = Rot([(A.alloc([128, 2, 128], F32), ('vst', i)) for i in range(2)])
        P_.xnb = Rot([(A.alloc([128, 1024], BF16), ('xnb', i)) for i in range(2)])
        P_.ott = Rot([(A.alloc([128, 8, 128], BF16), ('ott', i)) for i in range(2)])
        P_.sq = Rot([(A.alloc([128, 512], BF16), ('sq', i)) for i in range(1)])
        P_.qkn = Rot([(A.alloc([128, 512], F32), ('qkn', i)) for i in range(2)])
        P_.qkb = Rot([(A.alloc([128, 512], BF16), ('qkb', i)) for i in range(2)])
        P_.small = Rot([(A.alloc([128, 24], F32), ('small', i)) for i in range(6)])
        _of = P_.otok.rearrange("p a b -> p (a b)")
        P_.memt = _of[:, 0:4096].bitcast(F32).rearrange("p (a b) -> p a b", a=2)
        P_.xnTm = _of[:, 4096:6144].rearrange("p (c n) -> p c n", c=8)
        print("arena used (prompt):", A.off)
        P_.trb = Rot([1])
        P_.mmb = Rot([0])

        memset('pool', P_.Vaug[:, :, :, 64:65], 1.0, [], ['vaug_ones'])

        S.mute = SETUP < 5
        hkfull = P_.otok.rearrange("p a b -> p (a b)")[:, 0:8 * 640].rearrange("p (h n) -> p h n", h=8)
        for h in range(8):
            S.add('pool', lambda e, o=hkfull[:, h, :], i=bass.AP(tensor=rev_d.tensor, offset=h * 768, ap=[[1, 128], [1, 640]]):
                  e.dma_start(out=o, in_=i), ['rev_d'], ['hkfull'], dma=True)
        memset('pool', hkfull[64:128, :, 576:640], NEGM, ['hkfull'], ['hkfull'])
        memset('pool', hkfull[0:64, :, 0:64], NEGM, ['hkfull'], ['hkfull'])
        DMA('sp', hk_d.rearrange("h p n -> p h n"), hkfull, ['hkfull'], ['hk_d'] + [('ot', t) for t in range(16)])

        S.mute = False
        memset('pool', tmpf[:, 0:1], 0.0, [], SETUP_KEYS + [('xnT', 'p', t) for t in range(16)])
        REG = ('reg', 'AB')

        mixbanks = Rot([0])
        zbanks = Rot([2, 3, 4, 5])
        obanks = Rot([6, 7])
        bigmm = Rot([0, 2, 3, 4, 5, 6, 7])

        for b in range(NSEQ if STAGE >= 1 else 0):
            u = P_
            for t in range(16):
                DMA('sp', u.h[:, t, :], xp[b, t * 128:(t + 1) * 128, :], [], [('h', 'p', t)])
            for l in range(2 if DO_L1 else 1):
                STG = STAGE if l == 0 else STAGE1
                norm_unit(u, 0, l, mixbanks)
                if STG < 2:
                    break
                S.ctx = [REG]
                if l == 0:
                    for pi in range(8):
                        if pi < 4:
                            ok = lambda g0, ntt, pi=pi: (akp[b, (g0 - 12) * 128:(g0 - 12 + ntt) * 128, pi * 128:(pi + 1) * 128]
                                                         .rearrange("(a p) n -> p a n", p=128) if g0 >= 12 else None)
                            ov = lambda g0, ntt, pi=pi: (avp[b, (g0 - 12) * 128:(g0 - 12 + ntt) * 128, pi * 128:(pi + 1) * 128]
                                                         .rearrange("(a p) n -> p a n", p=128) if g0 >= 12 else None)
                            of = None
                        else:
                            ok = lambda g0, ntt, pi=pi: bkp[b, g0 * 128:(g0 + ntt) * 128, (pi - 4) * 128:(pi - 3) * 128].rearrange("(a p) n -> p a n", p=128)
                            ov = lambda g0, ntt, pi=pi: bvp[b, g0 * 128:(g0 + ntt) * 128, (pi - 4) * 128:(pi - 3) * 128].rearrange("(a p) n -> p a n", p=128)
                            of = bfp[b].rearrange("(a p) n -> p a n", p=128)
                        proj_pair(u, 'ab', pi, ok, ov, of, mixbanks)
                        if pi < 4:
                            if STG >= 3:
                                attn_A_prompt(u, pi, zbanks, obanks)
                        elif STG >= 4:
                            if pi == 4:
                                fox_cumsum(u)
                            attn_B_prompt(u, pi - 4, zbanks)
                    S.ctx = []
                    if STG < 5:
                        break
                    out_proj(u, wb_about, WK['about'], mixbanks)
                else:
                    for pi in range(8):
                        ok = lambda g0, ntt, pi=pi: ckp[b, g0 * 128:(g0 + ntt) * 128, pi * 128:(pi + 1) * 128].rearrange("(a p) n -> p a n", p=128)
                        ov = lambda g0, ntt, pi=pi: cvp[b, g0 * 128:(g0 + ntt) * 128, pi * 128:(pi + 1) * 128].rearrange("(a p) n -> p a n", p=128)
                        proj_pair(u, 'sb', pi, ok, ov, None, mixbanks)
                        if CL >= 2:
                            attn_C_prompt(u, pi, Rot([2, 3, 4, 5]))
                    S.ctx = []
                    if STG < 5:
                        break
                    out_proj(u, wb_sbout, WK['sbout'], mixbanks)
                if STG < 6:
                    break
                memset('dve', u.small.items[0][0][:, 0:1], 0.0, [u.small.items[0][1]], [REG, u.small.items[0][1]])
                S.ctx = [REG]
                DMA('sp', gx.rearrange("p a b -> p (a b)"), bass.AP(tensor=g_x.tensor, offset=l * 512, ap=[[0, 128], [1, 512]]), [], ['gx'])
                ts('pool', gx[:, 0, :], gx[:, 0, :], 1.0 / 16.0, ALU.mult, ['gx'], ['gx'])
                if XL >= 1:
                    mem_kv_prompt(u, l, b, mixbanks)
                S.mute = False
                if XL >= 2:
                    norm_unit(u, 1, l, mixbanks)
                if XL >= 3:
                    cross_attn_prompt(u, l, mixbanks, Rot([2, 3, 4, 5]))
                S.ctx = []
                out_proj(u, wb_xo[l], WK[('xo', l)], mixbanks)
                memset('pool', u.Vaug[:, :, :, 64:65], 1.0, [], [REG, 'vaug_ones'])
                if STG < 7:
                    break
                norm_unit(u, 2, l, mixbanks)
                mlp(u, l, bigmm)
            for t in range(16):
                DMA('pool', y_p[b, t * 128:(t + 1) * 128, :], u.h[:, t, :], [('h', 'p', t)], [])

        if DO_SAMPLE:
            S.ctx = []
            memset('pool', epst, EPS, [PHASE], [PHASE, 'epst'])
            A.off = base_off
            X = U()
            X.n = 's'; X.NT = 1
            X.h = A.alloc([128, 1, 1024], F32); X.xnT = A.alloc([128, 8, 128], BF16); X.otok = A.alloc([128, 1, 1024], BF16)
            X.h1 = A.alloc([128, 4, 128], BF16)
            ring_setup(A, 3)
            X.QT = A.alloc([128, 128], BF16); X.KT = A.alloc([128, 2176], BF16); X.Vaug = A.alloc([128, 17, 2, 65], BF16)
            X.KmT = A.alloc([128, 8, 256], BF16); X.Vm = A.alloc([128, 2, 4, 256], BF16); X.QxT = A.alloc([128, 8, 128], BF16)
            X.kc = Rot([(A.alloc([128, 16, 128], BF16), ('kc', i)) for i in range(2)])
            X.kmc = A.alloc([128, 2, 1024], BF16)
            X.ptp = [Rot([(A.alloc([128, 2, 128], BF16), ('ptp', b_, i)) for i in range(2)]) for b_ in range(4)]
            X.ptx = [Rot([(A.alloc([128, 128], BF16), ('ptx', b_, i)) for i in range(2)]) for b_ in range(4)]
            MA = A.alloc([128, 4, 32], BF16); MB = A.alloc([128, 4, 32], BF16); MC = A.alloc([128, 4, 32], BF16)
            MZ = A.alloc([128, 32], BF16)
            HsA = A.alloc([128, 4, 8, 32], BF16); Hn = A.alloc([128, 8, 128], BF16)
            X.lf = A.alloc([128, 1, 8], F32)
            CB = U(); CB.NT = 16
            CB.lf = A.alloc([128, 16, 8], F32); CB.pref = A.alloc([128, 17, 8], F32)
            CB.cT = A.alloc([128, 16, 8], F32); CB.car = A.alloc([128, 16, 8], F32)
            cTb = [A.alloc([128, 16, 8], F32) for _ in range(4)]
            ctotb = [A.alloc([128, 8], F32) for _ in range(4)]
            cend = [A.alloc([128, 8], F32) for _ in range(4)]
            bcolS = [A.alloc([128, 16, 8], F32) for _ in range(4)]
            bnew = [A.alloc([128, 8], F32) for _ in range(4)]
            cnew = A.alloc([128, 8], F32)
            SelB = [A.alloc([128, 128], F32) for _ in range(4)]
            SelR = [A.alloc([128, 128], F32) for _ in range(4)]
            BTf = A.alloc([128, 128], F32)
            tmpS = A.alloc([128, 4, 32], F32)
            X.sE = A.alloc([128, 2, 32], BF16); X.sSf = A.alloc([128, 2, 32], F32); X.sSb = A.alloc([128, 2, 32], BF16); X.sW = A.alloc([128, 2, 32], BF16)
            X.kst = Rot([(A.alloc([128, 2, 128], F32), ('kst', i)) for i in range(2)])
            X.vst = Rot([(A.alloc([128, 2, 128], F32), ('vst', i)) for i in range(2)])
            X.xnb = Rot([(A.alloc([128, 1024], BF16), ('xnb', i)) for i in range(2)])
            X.ott = Rot([(A.alloc([128, 8, 128], BF16), ('ott', i)) for i in range(2)])
            X.sq = Rot([(A.alloc([128, 512], BF16), ('sq', i)) for i in range(1)])
            X.qkn = Rot([(A.alloc([128, 512], F32), ('qkn', i)) for i in range(2)])
            X.qkb = Rot([(A.alloc([128, 512], BF16), ('qkb', i)) for i in range(2)])
            X.small = Rot([(A.alloc([128, 24], F32), ('small', i)) for i in range(6)])
            X.trb = Rot([1]); X.mmb = Rot([0]); CB.mmb = X.mmb
            X.koff = 2048; X.voff = 16
            print("arena used (sample):", A.off)
            smm = Rot([0])
            u = X
            memset('pool', X.Vaug[:, :, :, 64:65], 1.0, [], ['vaug_ones_s'])
            memset('pool', MZ, 0.0, [], ['MZ'])
            for b_ in range(4):
                for it_ in X.ptp[b_].items:
                    memset('pool', it_[0], 0.0, [], [it_[1]])
                for it_ in X.ptx[b_].items:
                    memset('pool', it_[0], 0.0, [], [it_[1]])
            for (Mt, key, cmpop) in ((MA, 'MA', None), (MB, 'MB', ALU.is_ge), (MC, 'MC', ALU.is_gt)):
                memset('pool', tmpS, 0.0, ['tmpS'], ['tmpS'])
                for b_ in range(4):
                    asel(tmpS[:, b_, :], tmpS[:, b_, :], [[0, 32]], ALU.is_ge, NEGM, -32 * b_, 1, ['tmpS'], ['tmpS'])
                    asel(tmpS[:, b_, :], tmpS[:, b_, :], [[0, 32]], ALU.is_ge, NEGM, 32 * b_ + 31, -1, ['tmpS'], ['tmpS'])
                    if cmpop is not None:
                        asel(tmpS[:, b_, :], tmpS[:, b_, :], [[1, 32]], cmpop, NEGM, 32 * b_, -1, ['tmpS'], ['tmpS'])
                cp('pool', Mt, tmpS, ['tmpS'], [key])
            for b_ in range(4):
                memset('pool', SelB[b_], 0.0, [], [('SelB', b_)])
                memset('pool', SelB[b_][0:1, 32 * b_:32 * b_ + 32], 1.0, [('SelB', b_)], [('SelB', b_)])
                memset('pool', SelR[b_], 0.0, [], [('SelR', b_)])
                asel(SelR[b_], SelR[b_], [[0, 128]], ALU.not_equal, 1.0, -(32 * b_ + 31), 1, [('SelR', b_)], [('SelR', b_)])
            memset('pool', BTf, 0.0, [], ['BTf'])
            for b_ in range(4):
                blk = BTf[:, 32 * b_:32 * b_ + 32]
                memset('pool', blk, 1.0, ['BTf'], ['BTf'])
                asel(blk, blk, [[0, 32]], ALU.is_ge, 0.0, -32 * b_, 1, ['BTf'], ['BTf'])
                asel(blk, blk, [[1, 32]], ALU.is_ge, 0.0, 32 * b_, -1, ['BTf'], ['BTf'])
            for i_ in range(4):
                S.add('pool', lambda e, o=HsA[:, i_, :, :], i=bass.AP(tensor=rev_d.tensor, offset=512 - 128 * i_, ap=[[1, 128], [768, 8], [1, 32]]):
                      e.dma_start(out=o, in_=i), ['rev_d'], ['HsA'], dma=True)
            S.add('pool', lambda e, o=Hn, i=bass.AP(tensor=rev_d.tensor, offset=0, ap=[[1, 128], [768, 8], [1, 128]]):
                  e.dma_start(out=o, in_=i), ['rev_d'], ['Hn'], dma=True)

            DMA('sp', u.h[:, 0, :], xs, [], [('h', 's', 0)])

            def load_cache_kv(ck, cv, b_, c0, ntile):
                kc, kck = u.kc.next()
                for t0 in range(0, ntile, 4):
                    S.add('pool', lambda e, o=kc[:, t0:t0 + 4, :], i=ck[b_, t0 * 128:(t0 + 4) * 128, c0:c0 + 128].rearrange("(a p) n -> p a n", p=128):
                          e.dma_start(out=o, in_=i), [], [kck], dma=True)
                for hh_ in range(2):
                    for t0 in range(0, ntile, 4):
                        S.add('pool', lambda e, o=u.Vaug[:, t0:t0 + 4, hh_, 0:64],
                              i=cv[b_, t0 * 128:(t0 + 4) * 128, c0 + hh_ * 64:c0 + hh_ * 64 + 64].rearrange("(a p) w -> p a w", p=128):
                              e.dma_start(out=o, in_=i), [('vaug', x) for x in range(t0, t0 + 4)], [('vaug', x) for x in range(t0, t0 + 4)], dma=True)
                for g8 in range(0, ntile, 8):
                    n8 = min(8, ntile - g8)
                    tb = u.trb.next()
                    pv = bank_bf(tb).rearrange("p (c n) -> p c n", c=8)
                    for c in range(n8):
                        tr(pv[:, c, :], kc[:, g8 + c, :], ident, [kck, 'ident'], [('ps', tb)])
                    cp('dve', u.KT[:, g8 * 128:(g8 + n8) * 128].rearrange("p (c n) -> p c n", c=n8), pv[:, 0:n8, :], [('ps', tb)],
                       [('kt', g8 + x) for x in range(n8)])

            def pv_acc(O, obk, b_, hh, P_ap, pk, vt, first, last):
                mm(O[:, hh, :], P_ap, u.Vaug[:, vt, hh, :], first, last, [pk, ('vaug', vt), 'vaug_ones_s'], [('ps', obk)], skip=True)

            def attn_sample_AB(kind, p):
                isA = kind == 'a'
                ntile = 4 if (isA or BX == 2) else 16
                ck, cv = (cak, cav) if isA else (cbk, cbv)
                obk = 6
                O = bank(obk)[:, 0:130].rearrange("p (g w) -> p g w", g=2)
                nmm = 0
                tot = 4 * (ntile + 1) * 2
                for b_ in range(4):
                    if not (BX == 5 and not isA):
                        load_cache_kv(ck, cv, b_, p * 128, ntile)
                    if BX == 6 and not isA:
                        continue
                    qs = slice(32 * b_, 32 * b_ + 32)
                    for i in (list(range(ntile)) if BX == 3 else ([ntile] if BX == 4 else list(range(ntile)) + [ntile])):
                        newt = i == ntile
                        zk = zs.next()
                        Z = bank(zk)[:, 0:64].rearrange("p (g w) -> p g w", g=2)
                        for hh in range(2):
                            ps_ = slice(hh * 64, (hh + 1) * 64)
                            hidx = 2 * p + hh
                            two = True
                            mm(Z[:, hh, :], u.KT[ps_, i * 128:(i + 1) * 128], u.QT[ps_, qs], True, (not isA) and (not newt) and (not two),
                               [('kt', i), ('qt', 0)], [('ps', zk)])
                            if two and (not isA) and (not newt):
                                mm(Z[:, hh, :], ident, MZ, False, True, ['ident', 'MZ'], [('ps', zk)])
                            if isA:
                                if newt:
                                    mm(Z[:, hh, :], Jbf, Hn[:, hidx, qs], False, False, ['Jbf', 'Hn'], [('ps', zk)])
                                    mm(Z[:, hh, :], ident, MA[:, b_, :], False, True, ['ident', 'MA'], [('ps', zk)])
                                else:
                                    mm(Z[:, hh, :], Jbf, HsA[:, i, hidx, :], False, True, ['Jbf', 'HsA'], [('ps', zk)])
                            elif newt:
                                mm(Z[:, hh, :], ident, MB[:, b_, :], False, True, ['ident', 'MB'], [('ps', zk)])
                        pt, ptk = u.ptp[b_].next()
                        if isA or BX in (1, 9):
                            act(pt[:, :, qs], Z, AF.Exp, [('ps', zk)], [ptk])
                        else:
                            for hh in range(2):
                                hidx = 2 * p + hh
                                bia = bnew[b_][:, hidx:hidx + 1] if newt else bcolS[b_][:, i, hidx:hidx + 1]
                                act(pt[:, hh, qs], Z[:, hh, :], AF.Exp, [('ps', zk), ('bcolS', b_)], [ptk], bias=bia)
                        for hh in range(2):
                            pv_acc(O, obk, b_, hh, pt[:, hh, :], ptk, i, nmm == 0, nmm >= tot - 2)
                            nmm += 1
                if BX == 6 and not isA:
                    return
                st, sk = u.small.next()
                S.add('dve', lambda e, o=st[:, 0:2], i_=O[:, :, 64]: e.reciprocal(out=o, in_=i_), [('ps', obk)], [sk])
                fo = (0 if isA else 512) + p * 128
                tt('dve', u.otok[:, 0, fo:fo + 128].rearrange("p (g w) -> p g w", g=2), O[:, :, 0:64],
                   st[:, 0:2].unsqueeze(2).to_broadcast([128, 2, 64]), ALU.mult, [('ps', obk), sk], [('ot', 0)])

            def fox_sample_prep():
                for b_ in range(4):
                    DMA('sp', CB.lf, cbf[b_].rearrange("(a p) h -> p a h", p=128), [], [('lf', t) for t in range(16)])
                    fox_cumsum(CB)
                    cp('dve', cTb[b_], CB.cT, [('cT', t) for t in range(16)], [('cTb', b_)])
                    cp('dve', ctotb[b_], CB.car[:, 15, :], [('car', 15)], [('ctotb', b_)])
                bk = smm.next()
                pp = bank(bk)
                mm(pp[:, 0:8], BTf, u.lf[:, 0, :], True, False, ['BTf', ('lf', 0)], [('ps', bk)])
                for b_ in range(4):
                    mm(pp[:, 0:8], SelB[b_], ctotb[b_], False, b_ == 3, [('SelB', b_), ('ctotb', b_)], [('ps', bk)])
                cp('dve', cnew, pp[:, 0:8], [('ps', bk)], ['cnew'])
                for b_ in range(4):
                    bk = smm.next()
                    pp = bank(bk)
                    mm(pp[:, 0:8], SelR[b_], cnew, True, True, [('SelR', b_), 'cnew'], [('ps', bk)])
                    cp('dve', cend[b_], pp[:, 0:8], [('ps', bk)], [('cend', b_)])
                    tt('dve', bcolS[b_], cend[b_].unsqueeze(1).to_broadcast([128, 16, 8]), cTb[b_], ALU.subtract,
                       [('cend', b_), ('cTb', b_)], [('bcolS', b_)])
                    tt('dve', bnew[b_], cend[b_], cnew, ALU.subtract, [('cend', b_), 'cnew'], [('bcolS', b_)])

            def attn_sample_C(p):
                obk = 6
                O = bank(obk)[:, 0:128].rearrange("p (g w) -> p g w", g=2)
                nmm = 0
                tot = 4 * 17 * 2
                E = u.sE; Sf = u.sSf; Sb = u.sSb; Wt = u.sW
                for b_ in range(4):
                    load_cache_kv(cck, ccv, b_, p * 128, 16)
                    qs = slice(32 * b_, 32 * b_ + 32)
                    for i in [16] + list(range(15, -1, -1)):
                        newt = i == 16
                        zk = zs.next()
                        Z = bank(zk)[:, 0:64].rearrange("p (g w) -> p g w", g=2)
                        rk_ = zs.next()
                        R = bank(rk_)[:, 0:64].rearrange("p (g w) -> p g w", g=2)
                        for hh in range(2):
                            ps_ = slice(hh * 64, (hh + 1) * 64)
                            mm(Z[:, hh, :], u.KT[ps_, i * 128:(i + 1) * 128], u.QT[ps_, qs], True, False, [('kt', i), ('qt', 0)], [('ps', zk)])
                            if newt:
                                mm(Z[:, hh, :], ident, MC[:, b_, :], False, True, ['ident', 'MC'], [('ps', zk)])
                            else:
                                mm(Z[:, hh, :], ident, MZ, False, True, ['ident', 'MZ'], [('ps', zk)])
                        act(E, Z, AF.Exp, [('ps', zk)], ['sE'])
                        act(E, E, AF.Ln, ['sE'], ['sE'], bias=1.0)
                        for hh in range(2):
                            ps_ = slice(hh * 64, (hh + 1) * 64)
                            mm(R[:, hh, :], u.KT[ps_, i * 128:(i + 1) * 128], u.QT[ps_, qs], True, False, [('kt', i), ('qt', 0)], [('ps', rk_)])
                            if newt:
                                mm(R[:, hh, :], ident, MC[:, b_, :], False, False, ['ident', 'MC'], [('ps', rk_)])
                            mm(R[:, hh, :], ntriL, E[:, hh, :], False, newt, ['ntriL', 'sE'], [('ps', rk_)])
                            if not newt:
                                mm(R[:, hh, :], nones, Sb[:, hh, :], False, True, ['nones', 'sSb'], [('ps', rk_)])
                        pt, ptk = u.ptp[b_].next()
                        act(pt[:, :, qs], R, AF.Exp, [('ps', rk_)], [ptk])
                        for hh in range(2):
                            mm(O[:, hh, :], pt[:, hh, :], u.Vaug[:, i, hh, 0:64], nmm == 0, nmm >= tot - 2,
                               [ptk, ('vaug', i)], [('ps', obk)], skip=True)
                            nmm += 1
                        if i > 0:
                            if newt:
                                cp('pool', Sf, E, ['sE'], ['sSf'])
                            else:
                                tt('pool', Sf, Sf, E, ALU.add, ['sE', 'sSf'], ['sSf'])
                            cp('pool', Sb, Sf, ['sSf'], ['sSb'])
                fo = p * 128
                cp('act', u.otok[:, 0, fo:fo + 128].rearrange("p (g w) -> p g w", g=2), O, [('ps', obk)], [('ot', 0)])

            def cross_sample(l):
                wv = wb_xq[l].rearrange("(c p) n -> p c n", p=128)
                Wq = []
                for half in range(2):
                    slot, rk = ring_load([(lambda s: s[:, 0:4096].rearrange("p (c n) -> p c n", c=8), wv[:, :, half * 512:(half + 1) * 512])], WK[('xq', l)])
                    Wq.append((slot[:, 0:4096].rearrange("p (c n) -> p c n", c=8), rk))
                cross_q_chunk(u, l, [0], Wq, smm)
                RS = bank(0)[:, 0:64].rearrange("p (x n) -> p x n", x=4)
                for b_ in range(4):
                    qs = slice(32 * b_, 32 * b_ + 32)
                    S.add('pool', lambda e, o=u.kmc, i=cmk[l, b_].rearrange("(a p) n -> p a n", p=128): e.dma_start(out=o, in_=i), [], ['kmc'], dma=True)
                    S.add('pool', lambda e, o=u.Vm.rearrange("p a h w -> p a (h w)"), i=cmv[l, b_].rearrange("(a p) n -> p a n", p=128):
                          e.dma_start(out=o, in_=i), [], ['vmS'], dma=True)
                    for mt in range(2):
                        tb = u.trb.next()
                        pv = bank_bf(tb).rearrange("p (c n) -> p c n", c=8)
                        for c in range(8):
                            tr(pv[:, c, :], u.kmc[:, mt, c * 128:(c + 1) * 128], ident, ['kmc', 'ident'], [('ps', tb)])
                        cp('dve', u.KmT[:, :, mt * 128:(mt + 1) * 128], pv, [('ps', tb)], ['kmtS'])
                    for h in range(4):
                        obk = 6 + h // 2
                        O = bank(obk)[:, (h % 2) * 256:(h % 2 + 1) * 256]
                        for mt in range(2):
                            zk = zs.next()
                            Z = bank(zk)[:, 0:32]
                            for dc in range(2):
                                mm(Z, u.KmT[:, h * 2 + dc, mt * 128:(mt + 1) * 128], u.QxT[:, h * 2 + dc, qs], dc == 0, dc == 1,
                                   ['kmtS', ('qxt', 0)], [('ps', zk)])
                            pt, ptk = u.ptx[b_].next()
                            act(pt[:, qs], Z, AF.Exp, [('ps', zk)], [ptk])
                            mm(O, pt, u.Vm[:, mt, h, :], b_ == 0 and mt == 0 and h % 2 == 0, b_ == 3 and mt == 1,
                               [ptk, 'vmS'], [('ps', obk)], skip=True)
                            mm(RS[:, h, :], pt, ones_bf[:, 0:16], b_ == 0 and mt == 0 and h == 0, b_ == 3 and mt == 1,
                               [ptk, 'ones_bf'], [('ps', 0)], skip=True)
                st, sk = u.small.next()
                S.add('dve', lambda e, o=st[:, 0:4], i_=RS[:, :, 0]: e.reciprocal(out=o, in_=i_), [('ps', 0)], [sk])
                for h in range(4):
                    obk = 6 + h // 2
                    O = bank(obk)[:, (h % 2) * 256:(h % 2 + 1) * 256]
                    ts('dve', u.otok[:, 0, h * 256:(h + 1) * 256], O, st[:, h:h + 1], ALU.mult, [('ps', obk), sk], [('ot', 0)])

            zs = Rot([2, 3, 4, 5])
            for l in range((2 if DO_L1 else 1) if SS >= 1 else 0):
                norm_unit(u, 0, l, smm)
                if l == 0:
                    for pi in range(8):
                        p = pi % 4
                        if pi < 4:
                            u.koff = 512; u.voff = 4
                            ok = lambda g0, ntt, pi=pi: aks[:, pi * 128:(pi + 1) * 128].rearrange("(a p) n -> p a n", p=128)
                            ov = lambda g0, ntt, pi=pi: avs[:, pi * 128:(pi + 1) * 128].rearrange("(a p) n -> p a n", p=128)
                            of = None
                        else:
                            u.koff = 2048; u.voff = 16
                            ok = lambda g0, ntt, pi=pi: bks[:, (pi - 4) * 128:(pi - 3) * 128].rearrange("(a p) n -> p a n", p=128)
                            ov = lambda g0, ntt, pi=pi: bvs[:, (pi - 4) * 128:(pi - 3) * 128].rearrange("(a p) n -> p a n", p=128)
                            of = bfs.rearrange("(a p) n -> p a n", p=128)
                        proj_pair(u, 'ab', pi, ok, ov, of, smm)
                        if pi == 4 and SS >= 3:
                            fox_sample_prep()
                        if (pi < 4 and SS >= 2) or (pi >= 4 and SS >= 4):
                            attn_sample_AB('a' if (pi < 4 or BX == 8) else 'b', p)
                    if SS < 5:
                        break
                    out_proj(u, wb_about, WK['about'], smm)
                else:
                    u.koff = 2048; u.voff = 16
                    for pi in range(8):
                        ok = lambda g0, ntt, pi=pi: cks[:, pi * 128:(pi + 1) * 128].rearrange("(a p) n -> p a n", p=128)
                        ov = lambda g0, ntt, pi=pi: cvs[:, pi * 128:(pi + 1) * 128].rearrange("(a p) n -> p a n", p=128)
                        proj_pair(u, 'sb', pi, ok, ov, None, smm)
                        attn_sample_C(pi)
                    out_proj(u, wb_sbout, WK['sbout'], smm)
                DMA('sp', gx.rearrange("p a b -> p (a b)"), bass.AP(tensor=g_x.tensor, offset=l * 512, ap=[[0, 128], [1, 512]]), [], ['gx'])
                ts('pool', gx[:, 0, :], gx[:, 0, :], 1.0 / 16.0, ALU.mult, ['gx'], ['gx'])
                if SS < 6:
                    break
                norm_unit(u, 1, l, smm)
                cross_sample(l)
                out_proj(u, wb_xo[l], WK[('xo', l)], smm)
                if SS < 7:
                    break
                norm_unit(u, 2, l, smm)
                mlp(u, l, Rot([0, 2, 3, 4, 5]))
            DMA('pool', y_s, u.h[:, 0, :], [('h', 's', 0)], [])

        S.finalize()
        print("ops:", len(S.ops), "signals:", S.cnt, "dmas:", S.dcnt)
        sems = {}
        for en in ('pe', 'act', 'dve', 'pool', 'sp'):
            sems[en] = [es.enter_context(nc.semaphore("s_%s_%d" % (en, i))) for i in range(S.n_epochs(en))]
        dsems = {}
        for q in ('sp', 'pool', 'act'):
            for j in range(S.NDMA):
                dsems[(q, j)] = es.enter_context(nc.semaphore("d_%s_%d" % (q, j)))
        block = es.enter_context(nc.Block())

        @block.tensor
        def _(e):
            S.emit('pe', e, sems, dsems)

        @block.scalar
        def _(e):
            S.emit('act', e, sems, dsems)

        @block.vector
        def _(e):
            S.emit('dve', e, sems, dsems)

        @block.gpsimd
        def _(e):
            S.emit('pool', e, sems, dsems)

        @block.sync
        def _(e):
            S.emit('sp', e, sems, dsems, final=True)
    return nc


_NC = None

IN_NAMES = ["xp", "xs", "memp", "cak", "cav", "cbk", "cbv", "cbf", "cck", "ccv", "cmk", "cmv", "g_norm", "w_abin", "b_f", "g_ab",
            "relb", "w_about", "w_sbin", "w_sbout", "w_xq", "w_xkv", "g_x", "w_xo", "w_up", "w_dn"]


def kernel(x_prompt, x_sample, mem_prompt, cache_a_k, cache_a_v, cache_b_k, cache_b_v, cache_b_logf,
           cache_c_k, cache_c_v, cache_mem_k, cache_mem_v, norm_mix, norm_cross, norm_mlp, norm_mem,
           ab_w_in, ab_forget_bias, ab_qk_gain, ab_rel_bias, ab_w_out, sb_w_in, sb_w_out,
           x_w_q, x_w_kv, x_qk_gain, x_w_o, mlp_w_up, mlp_w_down):
    global _NC
    f = lambda a: np.ascontiguousarray(np.asarray(a, dtype=np.float32))
    if _NC is None:
        _NC = build_program()
    nc = _NC
    g_norm = np.stack([f(norm_mix), f(norm_cross), f(norm_mlp), f(norm_mem)], 0).reshape(64, 128)
    shared = {
        "g_norm": f(g_norm), "w_abin": f(ab_w_in)[0], "b_f": f(ab_forget_bias)[0], "g_ab": f(ab_qk_gain)[0].reshape(256),
        "relb": f(ab_rel_bias)[0], "w_about": f(ab_w_out)[0], "w_sbin": f(sb_w_in)[0], "w_sbout": f(sb_w_out)[0],
        "w_xq": f(x_w_q), "w_xkv": f(x_w_kv), "g_x": f(x_qk_gain).reshape(1024), "w_xo": f(x_w_o),
        "w_up": f(mlp_w_up), "w_dn": f(mlp_w_down),
    }
    xp_ = f(x_prompt); xs_ = f(x_sample); mp_ = f(mem_prompt)
    in_maps = []
    for c in range(8):
        s = slice(4 * c, 4 * c + 4)
        m = dict(shared)
        m.update({
            "xp": np.ascontiguousarray(xp_[s]), "xs": np.ascontiguousarray(xs_[s].reshape(128, 1024)), "memp": np.ascontiguousarray(mp_[s]),
            "cak": f(cache_a_k[0, s]).reshape(4, 512, 512), "cav": f(cache_a_v[0, s]).reshape(4, 512, 512),
            "cbk": f(cache_b_k[0, s]).reshape(4, 2048, 512), "cbv": f(cache_b_v[0, s]).reshape(4, 2048, 512),
            "cbf": f(cache_b_logf[0, s]).reshape(4, 2048, 8),
            "cck": f(cache_c_k[0, s]).reshape(4, 2048, 1024), "ccv": f(cache_c_v[0, s]).reshape(4, 2048, 1024),
            "cmk": f(cache_mem_k[:, s]).reshape(2, 4, 256, 1024), "cmv": f(cache_mem_v[:, s]).reshape(2, 4, 256, 1024),
        })
        in_maps.append(m)
    res = run_bass_kernel_spmd(nc, in_maps, core_ids=list(range(8)))
    R = res.results

    def cat(name, shape_per_core, axis=0):
        return np.concatenate([np.asarray(R[c][name]).reshape(shape_per_core) for c in range(8)], axis=axis)

    y_prompt = cat("y_p", (4, 2048, 1024))
    y_sample = cat("y_s", (4, 32, 1024))
    a_k_p = cat("akp", (1, 4, 512, 8, 64), 1); a_v_p = cat("avp", (1, 4, 512, 8, 64), 1)
    a_k_s = cat("aks", (1, 4, 32, 8, 64), 1); a_v_s = cat("avs", (1, 4, 32, 8, 64), 1)
    b_k_p = cat("bkp", (1, 4, 2048, 8, 64), 1); b_v_p = cat("bvp", (1, 4, 2048, 8, 64), 1); b_f_p = cat("bfp", (1, 4, 2048, 8), 1)
    b_k_s = cat("bks", (1, 4, 32, 8, 64), 1); b_v_s = cat("bvs", (1, 4, 32, 8, 64), 1); b_f_s = cat("bfs", (1, 4, 32, 8), 1)
    c_k_p = cat("ckp", (1, 4, 2048, 16, 64), 1); c_v_p = cat("cvp", (1, 4, 2048, 16, 64), 1)
    c_k_s = cat("cks", (1, 4, 32, 16, 64), 1); c_v_s = cat("cvs", (1, 4, 32, 16, 64), 1)
    m_k_p = cat("mkp", (2, 4, 256, 4, 256), 1); m_v_p = cat("mvp", (2, 4, 256, 4, 256), 1)
    return (y_prompt, y_sample, a_k_p, a_v_p, a_k_s, a_v_s, b_k_p, b_v_p, b_f_p, b_k_s, b_v_s, b_f_s,
            c_k_p, c_v_p, c_k_s, c_v_s, m_k_p, m_v_p)
```
